# Optimizing a Trainium2 kernel written in Bass

```python
import jax
import jax.numpy as jnp
from jax import lax
import numpy as np

D_MODEL = 2048
BATCH = 4
SEQ = 2048
DEPTH = 1

CTX_LEN = 256
GRID_W = 64
GLA_HEADS = 4
GLA_DK = 128
GLA_DV = 256
GLA_GATE_RANK = 16
GLA_GATE_TAU = 16.0
GLA_CHUNK = 64
FNET_GROUPS = 4
FNET_GROUP_DIM = 256
D_FF = 5632
CONV_W = 3
EPS = 1e-6

QK_W = GLA_HEADS * GLA_DK
V_W = GLA_HEADS * GLA_DV
FNET_W = FNET_GROUPS * FNET_GROUP_DIM
IN_SIZES = (QK_W, QK_W, V_W, V_W, GLA_GATE_RANK, GLA_GATE_RANK, FNET_W, D_MODEL, D_MODEL)
IN_OFFSETS = tuple(int(o) for o in np.cumsum((0,) + IN_SIZES))
IN_W = IN_OFFSETS[-1]

kernel_name = 'hybrid_gla_fnet_convffn_dit'


def rmsnorm(x, g):
    xf = x.astype(jnp.float32)
    y = xf * lax.rsqrt(jnp.mean(xf * xf, axis=-1, keepdims=True) + EPS)
    return (y * g.astype(jnp.float32)).astype(x.dtype)


def adaln(cond, w, b):
    mod = jax.nn.silu(cond) @ w + b
    return [m[:, None, :] for m in jnp.split(mod, 6, axis=-1)]


def modulate(h, shift, scale):
    return h * (1 + scale) + shift


def to_heads(t):
    bsz, T, _ = t.shape
    return t.reshape(bsz, T, GLA_HEADS, -1).transpose(0, 2, 1, 3).astype(jnp.float32)


def flip_time(t):
    return jnp.flip(t, axis=2)


def gla_log_decay(lr, w, b):
    return to_heads(jax.nn.log_sigmoid((lr @ w + b).astype(jnp.float32)) / GLA_GATE_TAU)


def gla_chunked(q, k, v, log_a, s0):
    bsz, H, T, dk = q.shape
    C = GLA_CHUNK
    N = T // C
    q, k, v, log_a = (t.reshape(bsz, H, N, C, t.shape[-1]) for t in (q, k, v, log_a))
    b = jnp.cumsum(log_a, axis=3)
    b_last = b[:, :, :, -1:, :]
    q_dec = q * jnp.exp(b)
    k_dec = k * jnp.exp(-b)
    k_state = k * jnp.exp(b_last - b)
    mask = jnp.tril(jnp.ones((C, C), dtype=bool))
    scores = jnp.where(mask, jnp.einsum('bhncd,bhnsd->bhncs', q_dec, k_dec), 0.0)
    o_intra = jnp.einsum('bhncs,bhnsv->bhncv', scores, v)
    kv = jnp.einsum('bhncd,bhncv->nbhdv', k_state, v)
    decay = jnp.moveaxis(jnp.exp(b_last[:, :, :, 0, :]), 2, 0)

    def step(s, inp):
        dec, kv_n = inp
        return dec[..., None] * s + kv_n, s

    s_final, s_prev = lax.scan(step, s0, (decay, kv))
    o_inter = jnp.einsum('bhncd,nbhdv->bhncv', q_dec, s_prev)
    return (o_intra + o_inter).reshape(bsz, H, T, -1), s_final


def gla_final_state(k, v, log_a):
    b = jnp.cumsum(log_a, axis=2)
    w = jnp.exp(b[:, :, -1:, :] - b)
    return jnp.einsum('bhtd,bhtv->bhdv', k * w, v)


def token_mixer(h, w_in, w_gate_f, b_gate_f, w_gate_b, b_gate_b, g_gla,
                w_gla_out, w_fnet_out, w_out, s0_f, s0_b):
    bsz, T, _ = h.shape
    proj = h @ w_in
    q, k, v, r, lr_f, lr_b, f_in, g_a, g_b = jnp.split(proj, IN_OFFSETS[1:-1], axis=-1)
    qh = to_heads(q) * (GLA_DK ** -0.5)
    kh, vh = to_heads(k), to_heads(v)
    la_f = gla_log_decay(lr_f, w_gate_f, b_gate_f)
    la_b = gla_log_decay(lr_b, w_gate_b, b_gate_b)
    o_f, s_f = gla_chunked(qh, kh, vh, la_f, s0_f)
    o_b, s_b = gla_chunked(flip_time(qh), flip_time(kh), flip_time(vh), flip_time(la_b), s0_b)
    o = o_f + flip_time(o_b)
    o = o * lax.rsqrt(jnp.mean(o * o, axis=-1, keepdims=True) + EPS)
    o = o.transpose(0, 2, 1, 3).reshape(bsz, T, V_W) * g_gla.astype(jnp.float32)
    y_a = (jax.nn.silu(r.astype(jnp.float32)) * o).astype(h.dtype) @ w_gla_out
    f = f_in.astype(jnp.float32).reshape(bsz, T, FNET_GROUPS, FNET_GROUP_DIM)
    f = jnp.fft.fft2(f, axes=(1, 3), norm='ortho').real.reshape(bsz, T, FNET_W)
    y_b = f.astype(h.dtype) @ w_fnet_out
    y = jax.nn.sigmoid(g_a) * y_a + jax.nn.sigmoid(g_b) * y_b
    return y @ w_out, s_f, s_b


def context_gla_states(hc, w_in, w_gate_f, b_gate_f, w_gate_b, b_gate_b):
    def cols(i):
        return hc @ w_in[:, IN_OFFSETS[i]:IN_OFFSETS[i + 1]]
    kh, vh = to_heads(cols(1)), to_heads(cols(2))
    la_f = gla_log_decay(cols(4), w_gate_f, b_gate_f)
    la_b = gla_log_decay(cols(5), w_gate_b, b_gate_b)
    s_f = gla_final_state(kh, vh, la_f)
    s_b = gla_final_state(flip_time(kh), flip_time(vh), flip_time(la_b))
    return s_f, s_b


def dwconv_rows(u, w, b, n_rows, row_len):
    bsz, T, ch = u.shape
    p = jnp.pad(u.reshape(bsz, n_rows, row_len, ch), ((0, 0), (0, 0), (1, 1), (0, 0)))
    y = p[:, :, :-2] * w[0] + p[:, :, 1:-1] * w[1] + p[:, :, 2:] * w[2] + b
    return y.reshape(bsz, T, ch)


def conv_ffn(h, w_up, conv_w, conv_b, w_down, n_rows, row_len):
    u = dwconv_rows(h @ w_up, conv_w, conv_b, n_rows, row_len)
    val, gate = jnp.split(u, 2, axis=-1)
    return (jax.nn.silu(gate) * val) @ w_down


def setup_inputs(seed: int = 0) -> dict:
    key = jax.random.key(seed)
    ks = jax.random.split(key, 24)
    L, D, F2 = DEPTH, D_MODEL, 2 * D_FF

    def nrm(k, shape, fan):
        return jax.random.normal(k, shape, jnp.float32) * (fan ** -0.5)

    def gain(k, shape):
        return 1.0 + 0.02 * jax.random.normal(k, shape, jnp.float32)

    def bias(k, shape):
        return 0.02 * jax.random.normal(k, shape, jnp.float32)

    return {
        'x': jax.random.normal(ks[0], (BATCH, SEQ, D), jnp.float32),
        'c': jax.random.normal(ks[1], (BATCH, D), jnp.float32),
        'ctx': jax.random.normal(ks[2], (BATCH, CTX_LEN, D), jnp.float32),
        'c_ctx': jax.random.normal(ks[3], (D,), jnp.float32),
        'w_ada': nrm(ks[4], (L, D, 6 * D), D),
        'b_ada': bias(ks[5], (L, 6 * D)),
        'g_norm1': gain(ks[6], (L, D)),
        'w_in': nrm(ks[7], (L, D, IN_W), D),
        'w_gate_f': nrm(ks[8], (L, GLA_GATE_RANK, QK_W), GLA_GATE_RANK),
        'b_gate_f': bias(ks[9], (L, QK_W)),
        'w_gate_b': nrm(ks[10], (L, GLA_GATE_RANK, QK_W), GLA_GATE_RANK),
        'b_gate_b': bias(ks[11], (L, QK_W)),
        'g_gla': gain(ks[12], (L, V_W)),
        'w_gla_out': nrm(ks[13], (L, V_W, D), V_W),
        'w_fnet_out': nrm(ks[14], (L, FNET_W, D), FNET_W),
        'w_out': nrm(ks[15], (L, D, D), D),
        'g_norm2': gain(ks[16], (L, D)),
        'w_up': nrm(ks[17], (L, D, F2), D),
        'conv_w': nrm(ks[18], (L, CONV_W, F2), CONV_W),
        'conv_b': bias(ks[19], (L, F2)),
        'w_down': nrm(ks[20], (L, D_FF, D), D_FF),
        'g_final': gain(ks[21], (D,)),
    }


def reference(x, c, ctx, c_ctx, w_ada, b_ada, g_norm1, w_in, w_gate_f, b_gate_f,
              w_gate_b, b_gate_b, g_gla, w_gla_out, w_fnet_out, w_out, g_norm2,
              w_up, conv_w, conv_b, w_down, g_final):
    rows = x.shape[1] // GRID_W
    ctx_len = ctx.shape[1]
    for l in range(DEPTH):
        sh1, sc1, gt1, sh2, sc2, gt2 = adaln(c, w_ada[l], b_ada[l])
        csh1, csc1, cgt1, csh2, csc2, cgt2 = adaln(c_ctx[None, :], w_ada[l], b_ada[l])
        mix_w = (w_in[l], w_gate_f[l], b_gate_f[l], w_gate_b[l], b_gate_b[l], g_gla[l],
                 w_gla_out[l], w_fnet_out[l], w_out[l])
        hc = modulate(rmsnorm(ctx, g_norm1[l]), csh1, csc1)
        if l < DEPTH - 1:
            zeros = jnp.zeros((ctx.shape[0], GLA_HEADS, GLA_DK, GLA_DV), jnp.float32)
            yc, s_f, s_b = token_mixer(hc, *mix_w, zeros, zeros)
            ctx = ctx + cgt1 * yc
            hc2 = modulate(rmsnorm(ctx, g_norm2[l]), csh2, csc2)
            ctx = ctx + cgt2 * conv_ffn(hc2, w_up[l], conv_w[l], conv_b[l], w_down[l], 1, ctx_len)
        else:
            s_f, s_b = context_gla_states(hc, w_in[l], w_gate_f[l], b_gate_f[l],
                                          w_gate_b[l], b_gate_b[l])
        hx = modulate(rmsnorm(x, g_norm1[l]), sh1, sc1)
        yx, _, _ = token_mixer(hx, *mix_w, s_f, s_b)
        x = x + gt1 * yx
        hx2 = modulate(rmsnorm(x, g_norm2[l]), sh2, sc2)
        x = x + gt2 * conv_ffn(hx2, w_up[l], conv_w[l], conv_b[l], w_down[l], rows, GRID_W)
    return rmsnorm(x, g_final)
```

```python
import bisect
from contextlib import ExitStack

import ml_dtypes
import numpy as np

import concourse.bass as bass
import concourse.mybir as mybir
from concourse.bass_utils import run_bass_kernel_spmd

F32 = mybir.dt.float32
BF16 = mybir.dt.bfloat16
AF = mybir.ActivationFunctionType
ALU = mybir.AluOpType

D = 2048
T_OWN = 1024
T_REST = 1280
NH = 4
DK = 128
DV = 256
D_FF = 5632
EPS = 1e-6
IN_SIZES = (512, 512, 1024, 1024, 16, 16, 1024, 2048, 2048)
OFF = [int(o) for o in np.cumsum((0,) + IN_SIZES)]
O_Q, O_K, O_V, O_R, O_LF, O_LB, O_F, O_GA, O_GB = OFF[:9]
IN_W = OFF[-1]
WB = 4096


class Tok:
    __slots__ = ("eng", "seq", "sem", "val")

    def __init__(self, eng=None, seq=0, sem=None, val=0):
        self.eng, self.seq, self.sem, self.val = eng, seq, sem, val


class T:
    def __init__(self, k, ap, name):
        self.k, self.ap, self.name = k, ap, name
        self.last_w = None
        self.reads = {}
        self.dsem = None
        self.dcnt = 0

    def __getitem__(self, key):
        return self.ap[key]


class Eng:
    def __init__(self, k, name, eng):
        self.k, self.name, self.eng = k, name, eng
        self.sem = k.new_sem("e_" + name)
        self.seq = 0
        self.cnt = 0
        self.mseq = []
        self.mcnt = []
        self.last = None
        self.last_marked = True
        self.waited = {}


class Kern:
    def __init__(self, nc, es):
        self.nc, self.es = nc, es
        self.nsem = 0
        self.pe = Eng(self, "pe", nc.tensor)
        self.act = Eng(self, "act", nc.scalar)
        self.dve = Eng(self, "dve", nc.vector)
        self.pool = Eng(self, "pool", nc.gpsimd)
        self.sp = Eng(self, "sp", nc.sync)
        self.all_t = []

    def new_sem(self, name):
        self.nsem += 1
        return self.es.enter_context(self.nc.semaphore(f"{name}_{self.nsem}"))

    def sb(self, es, name, shape, dt):
        h = es.enter_context(self.nc.sbuf_tensor(name, list(shape), dt))
        return T(self, h[:], name)

    def view(self, ap, name="v"):
        return T(self, ap, name)

    def init_arena(self, nbytes):
        self.arena_h = self.es.enter_context(self.nc.sbuf_tensor("arena", [128, nbytes // 4], F32))
        self.free_list = [(0, nbytes)]
        self.live = {}
        self.dirty = False
        self.bar_sem = self.new_sem("bar")
        self.nbar = 0

    def alloc(self, name, shape, dt):
        if self.dirty:
            self.barrier()
            self.dirty = False
        esz = 2 if dt == BF16 else 4
        per = int(np.prod(shape[1:])) * esz
        nb = (per + 63) // 64 * 64
        for idx, (o, sz) in enumerate(self.free_list):
            if sz >= nb:
                self.free_list[idx] = (o + nb, sz - nb)
                break
        else:
            raise RuntimeError(f"arena OOM for {name} {shape} need {nb} free {self.free_list}")
        ap = self.arena_h[0:shape[0], o // 4:(o + per) // 4]
        if dt == BF16:
            ap = ap.bitcast(BF16)
        if len(shape) > 2:
            names = [f"d{i}" for i in range(len(shape) - 1)]
            kw = {n: s for n, s in zip(names[1:], shape[2:])}
            ap = ap.rearrange("p (" + " ".join(names) + ") -> p " + " ".join(names), **kw)
        t = T(self, ap, name)
        self.live[id(t)] = (o, nb)
        return t

    def free(self, *ts):
        for t in ts:
            o, nb = self.live.pop(id(t))
            self.free_list.append((o, nb))
        fl = sorted(x for x in self.free_list if x[1] > 0)
        out = []
        for o, sz in fl:
            if out and out[-1][0] + out[-1][1] == o:
                out[-1] = (out[-1][0], out[-1][1] + sz)
            else:
                out.append((o, sz))
        self.free_list = out
        self.dirty = True

    def barrier(self):
        comp = [self.pe, self.act, self.dve, self.pool]
        for E in comp:
            if E.last is not None:
                self.mark(E)
        SP = self.sp
        for E in comp:
            if E.cnt > 0 and SP.waited.get(id(E.sem), 0) < E.cnt:
                SP.eng.wait_ge(E.sem, E.cnt)
                SP.waited[id(E.sem)] = E.cnt
        for t in self.all_t:
            if t.dsem is not None and SP.waited.get(id(t.dsem), 0) < t.dcnt:
                SP.eng.wait_ge(t.dsem, t.dcnt)
                SP.waited[id(t.dsem)] = t.dcnt
        SP.eng.sem_inc(self.bar_sem, 1)
        self.nbar += 1
        for E in comp:
            E.eng.wait_ge(self.bar_sem, self.nbar)

    def mark(self, E):
        if E.last_marked:
            return
        E.last.then_inc(E.sem, 1)
        E.cnt += 1
        E.mseq.append(E.seq)
        E.mcnt.append(E.cnt)
        E.last_marked = True

    def resolve(self, tok):
        if tok.sem is not None:
            return tok.sem, tok.val
        E = tok.eng
        i = bisect.bisect_left(E.mseq, tok.seq)
        if i >= len(E.mseq):
            self.mark(E)
            i = len(E.mseq) - 1
        return E.sem, E.mcnt[i]

    def _wait(self, E, tok):
        if tok is None:
            return
        if tok.sem is None and tok.eng is E and E is self.pe:
            return
        sem, val = self.resolve(tok)
        key = id(sem)
        if E.waited.get(key, 0) >= val:
            return
        E.eng.wait_ge(sem, val)
        E.waited[key] = val

    def deps(self, E, reads, writes, join=False):
        for t in reads:
            self._wait(E, t.last_w)
        for t in writes:
            if not (join and t.last_w is not None and t.last_w.sem is not None):
                self._wait(E, t.last_w)
            for tok in t.reads.values():
                self._wait(E, tok)

    def op(self, E, fn, reads=(), writes=(), mark=True):
        self.deps(E, reads, writes)
        ins = fn()
        E.seq += 1
        E.last = ins
        E.last_marked = False
        tok = Tok(E, E.seq)
        if mark:
            self.mark(E)
        for t in writes:
            t.last_w = tok
            t.reads = {}
        for t in reads:
            t.reads[E.name] = tok
        return ins

    def dma(self, Q, dst_t, dst_ap, src_t, src_ap, join=False):
        self.deps(Q, [src_t], [dst_t], join=join)
        if dst_t.dsem is None:
            dst_t.dsem = self.new_sem("d_" + dst_t.name)
            self.all_t.append(dst_t)
        dst_t.dcnt += 16
        Q.eng.dma_start(out=dst_ap, in_=src_ap).then_inc(dst_t.dsem, 16)
        tok = Tok(None, 0, dst_t.dsem, dst_t.dcnt)
        dst_t.last_w = tok
        dst_t.reads = {}
        src_t.reads["dma%d" % id(dst_t.dsem)] = tok

    def mm(self, out_t, out_ap, l_t, l_ap, r_t, r_ap, start, stop, mark=None):
        if mark is None:
            mark = stop
        rd = [l_t, r_t]
        return self.op(self.pe, lambda: self.nc.tensor.matmul(out_ap, l_ap, r_ap, start=start, stop=stop),
                       rd, [out_t], mark=mark)

    def tr(self, out_t, out_ap, in_t, in_ap, id_t, id_ap, mark=True):
        return self.op(self.pe, lambda: self.nc.tensor.transpose(out_ap, in_ap, id_ap), [in_t, id_t], [out_t], mark=mark)

    def actf(self, out_t, out_ap, in_t, in_ap, func, bias=None, scale=None, accum=None, extra_r=(), extra_w=()):
        kw = {}
        if bias is not None:
            kw["bias"] = bias
        if scale is not None:
            kw["scale"] = scale
        if accum is not None:
            kw["accum_out"] = accum
        return self.op(self.act, lambda: self.nc.scalar.activation(out=out_ap, in_=in_ap, func=func, **kw),
                       [in_t] + list(extra_r), [out_t] + list(extra_w))

    def tt(self, E, out_t, out_ap, a_t, a_ap, b_t, b_ap, op):
        return self.op(E, lambda: E.eng.tensor_tensor(out=out_ap, in0=a_ap, in1=b_ap, op=op), [a_t, b_t], [out_t])

    def ts(self, E, out_t, out_ap, a_t, a_ap, s1, s2, op0, op1=None, extra_r=()):
        if op1 is None:
            f = lambda: E.eng.tensor_scalar(out=out_ap, in0=a_ap, scalar1=s1, scalar2=None, op0=op0)
        else:
            f = lambda: E.eng.tensor_scalar(out=out_ap, in0=a_ap, scalar1=s1, scalar2=s2, op0=op0, op1=op1)
        return self.op(E, f, [a_t] + list(extra_r), [out_t])

    def stt(self, out_t, out_ap, a_t, a_ap, sc, b_t, b_ap, op0, op1, extra_r=()):
        return self.op(self.dve, lambda: self.nc.vector.scalar_tensor_tensor(out=out_ap, in0=a_ap, scalar=sc, in1=b_ap,
                                                                            op0=op0, op1=op1),
                       [a_t, b_t] + list(extra_r), [out_t])

    def copy(self, E, out_t, out_ap, in_t, in_ap):
        if E is self.act:
            return self.op(E, lambda: self.nc.scalar.copy(out=out_ap, in_=in_ap), [in_t], [out_t])
        return self.op(E, lambda: E.eng.tensor_copy(out=out_ap, in_=in_ap), [in_t], [out_t])


class Ring:
    def __init__(self, items):
        self.items, self.i = items, 0

    def get(self):
        t = self.items[self.i % len(self.items)]
        self.i += 1
        return t


def build(taps=None, stop_after=None):
    taps = taps or []
    nc = bass.Bass("TRN2", target_bir_lowering=False)
    es = ExitStack()
    k = Kern(nc, es)
    PE, ACT, DVE, POOL, SP = k.pe, k.act, k.dve, k.pool, k.sp

    def din(name, shape, dt=F32):
        return T(k, nc.dram_tensor(name, list(shape), dt, kind="ExternalInput").ap(), name)

    def dscr(name, shape, dt=F32):
        return T(k, nc.dram_tensor(name, list(shape), dt, kind="Internal").ap(), name)

    xo = din("xo", [T_OWN, D])
    xr = din("xr", [T_REST, D])
    cvec = din("cvec", [2, D])
    w_ada = din("w_ada", [D, 6 * D])
    b_ada = din("b_ada", [1, 6 * D])
    w_in = din("w_in", [D, IN_W])
    wgate = din("wgate", [2, 16, 512])
    bgate = din("bgate", [2, 512])
    vtab = din("vtab", [128, 128])
    cwtab = din("cwtab", [3, 88, 128])
    g_final = din("g_final", [1, D])
    w_gla_out = din("w_gla_out", [1024, D])
    w_fnet_out = din("w_fnet_out", [1024, D])
    w_out = din("w_out", [D, D])
    w_up = din("w_up", [D, 2 * D_FF])
    w_down = din("w_down", [D_FF, D])
    c_identf = din("c_identf", [128, 128])
    c_identb = din("c_identb", [128, 128], BF16)
    c_tri = din("c_tri", [128, 4, 128])
    c_mask = din("c_mask", [128, 2, 128])
    c_mask8 = din("c_mask8", [128, 8 * 256], BF16)
    c_dftc = din("c_dftc", [128, 2, 2, 256], BF16)
    c_ct = din("c_ct", [2048, 1024], BF16)
    c_st = din("c_st", [2048, 1024], BF16)
    c_flags = din("c_flags", [128, 2])
    out_d = T(k, nc.dram_tensor("out", [T_OWN, D], F32, kind="ExternalOutput").ap(), "out")
    mod_d = dscr("mod_d", [2, 6 * D])
    x1_d = dscr("x1_d", [T_OWN, D])
    x2_d = dscr("x2_d", [T_OWN, D])
    tap_outs = []

    def tap(name, t, ap, shape, dt=F32):
        if name not in taps:
            return
        o = T(k, nc.dram_tensor("tap_" + name, list(shape), dt, kind="ExternalOutput").ap(), "tap_" + name)
        k.dma(SP, o, o.ap, t, ap)
        tap_outs.append(o)

    def finish():
        for o in [out_d] + tap_outs:
            if o.dsem is not None:
                SP.eng.wait_ge(o.dsem, o.dcnt)
        es.close()
        return nc

    ps_h = es.enter_context(nc.psum_tensor("ps", [128, 4096], F32))
    banks = [T(k, ps_h[:, i * 512:(i + 1) * 512], f"bank{i}") for i in range(8)]
    bank_ptr = [0]

    def getn(n):
        p = (bank_ptr[0] + n - 1) // n * n
        if p + n > 8:
            p = 0
        bank_ptr[0] = p + n
        return banks[p:p + n], ps_h[:, p * 512:(p + n) * 512]

    class _PSR:
        def get(self):
            return getn(1)[0][0]
    psr = _PSR()
    psb_t = [banks[6], banks[7]]
    psb_ap = ps_h[:, 3072:4096].bitcast(BF16)

    identf = k.sb(es, "identf", [128, 128], F32)
    identb = k.sb(es, "identb", [128, 128], BF16)
    tri = k.sb(es, "tri", [128, 4, 128], F32)
    mask = k.sb(es, "mask", [128, 2, 128], F32)
    flags = k.sb(es, "flags", [128, 2], F32)
    ones_r = k.sb(es, "ones_r", [1, 128], F32)
    epsc = k.sb(es, "epsc", [128, 1], F32)
    vecs = k.sb(es, "vecs", [128, 128], F32)
    cw = k.sb(es, "cw", [128, 3, 88], F32)
    modT = k.sb(es, "modT", [128, 96], F32)
    cmodT = k.sb(es, "cmodT", [128, 32], F32)
    AB = k.sb(es, "AB", [128, 6, 16], F32)
    wgs = k.sb(es, "wgs", [16, 2, 512], F32)
    bgs = k.sb(es, "bgs", [1, 2, 512], F32)
    small = k.sb(es, "small", [128, 8], F32)
    k.init_arena(190 * 1024)
    wbufs4 = [k.alloc(f"wb{i}", [128, WB], BF16) for i in range(4)]
    wring = Ring(wbufs4)
    hTo = k.alloc("hTo", [128, 16, T_OWN], BF16)

    for t, d in ((identf, c_identf), (identb, c_identb), (tri, c_tri), (mask, c_mask), (flags, c_flags)):
        k.dma(SP, t, t.ap, d, d.ap)
    k.dma(SP, wgs, wgs.ap, wgate, wgate.ap.rearrange("a r n -> r a n"))
    k.dma(SP, bgs, bgs.ap, bgate, bgate.ap.rearrange("(o a) n -> o a n", o=1))
    k.op(DVE, lambda: nc.vector.memset(ones_r.ap, 1.0), [], [ones_r])
    k.op(DVE, lambda: nc.vector.memset(epsc.ap, EPS), [], [epsc])

    def wload(src_t, src3, KT, n):
        b = wring.get()
        v = b.ap[:, 0:KT * n].rearrange("p (kt n) -> p kt n", n=n)
        k.dma(POOL, b, v, src_t, src3)
        return b, v

    def wsrc(w_t, KT, c0, n, r0=0):
        return w_t.ap[r0:r0 + KT * 128, :].rearrange("(kt p) n -> p kt n", p=128)[:, :, c0:c0 + n]

    tb = k.alloc("tb", [128, 128], F32)

    def load_T(src_t, src_ap, rows, dst_t, dst_ap):
        k.dma(SP, tb, tb.ap[0:rows, :], src_t, src_ap)
        b = psr.get()
        k.tr(b, b.ap[:, 0:rows], tb, tb.ap[0:rows, :], identf, identf.ap[0:rows, 0:rows])
        k.copy(DVE, dst_t, dst_ap, b, b.ap[:, 0:rows])

    load_T(vtab, vtab.ap, 128, vecs, vecs.ap)
    for j in range(3):
        load_T(cwtab, cwtab.ap[j], 88, cw, cw.ap[:, j, :])

    cs = k.alloc("cs", [128, 2, 16], F32)
    csb = k.alloc("csb", [128, 2, 16], BF16)
    barow = k.alloc("barow", [1, 6 * D], F32)
    mrow = k.alloc("mrow", [1, 6 * D], F32)
    crow = k.alloc("crow", [1, 2 * D], F32)
    for r in range(2):
        k.dma(SP, cs, cs.ap[:, r, :], cvec, cvec.ap[r].rearrange("(p k) -> p k", k=16), join=True)
    k.dma(SP, barow, barow.ap, b_ada, b_ada.ap)
    k.actf(csb, csb.ap, cs, cs.ap, AF.Silu)
    wa3 = w_ada.ap.rearrange("(p k) n -> p k n", k=16)
    for g in range(48):
        wb, wv = wload(w_ada, wa3[:, :, g * 256:(g + 1) * 256], 16, 256)
        for r in range(2 if g < 16 else 1):
            b = psr.get()
            for kk in range(16):
                k.mm(b, b.ap[0:1, 0:256], csb, csb.ap[:, r, kk:kk + 1], wb, wv[:, kk, :], kk == 0, kk == 15)
            dst = mrow if r == 0 else crow
            k.tt(DVE, dst, dst.ap[0:1, g * 256:(g + 1) * 256], b, b.ap[0:1, 0:256],
                 barow, barow.ap[0:1, g * 256:(g + 1) * 256], ALU.add)
    k.dma(SP, mod_d, mod_d.ap[0:1, :], mrow, mrow.ap, join=True)
    k.dma(SP, mod_d, mod_d.ap[1:2, 0:2 * D], crow, crow.ap, join=True)
    load_T(mod_d, mod_d.ap[0].rearrange("(j p) -> j p", p=128), 96, modT, modT.ap)
    load_T(mod_d, mod_d.ap[1, 0:2 * D].rearrange("(j p) -> j p", p=128), 32, cmodT, cmodT.ap)

    def mkAB(ia, ib, gcol, m_t, sh0, sc0):
        k.ts(DVE, AB, AB.ap[:, ia, :], m_t, m_t.ap[:, sc0:sc0 + 16], 1.0, None, ALU.add)
        k.tt(DVE, AB, AB.ap[:, ia, :], AB, AB.ap[:, ia, :], vecs, vecs.ap[:, gcol:gcol + 16], ALU.mult)
        k.copy(DVE, AB, AB.ap[:, ib, :], m_t, m_t.ap[:, sh0:sh0 + 16])
    mkAB(0, 1, 0, modT, 0, 16)
    mkAB(2, 3, 0, cmodT, 0, 16)
    mkAB(4, 5, 16, modT, 48, 64)
    k.free(tb, cs, csb, barow, mrow, crow)
    tap("modT", modT, modT.ap, [128, 96])
    tap("AB", AB, AB.ap, [128, 6, 16])
    if stop_after == "A":
        return finish()

    def build_hT(src_t, ntiles, hT, ab_of_tile):
        xb = [k.alloc(f"xbuf{i}", [128, D], F32) for i in range(2)]
        xsb = [k.alloc(f"xsb{i}", [128, D], BF16) for i in range(2)]
        for i in range(ntiles):
            xt = xb[i % 2]
            xs = xsb[i % 2]
            k.dma(SP, xt, xt.ap, src_t, src_t.ap[i * 128:(i + 1) * 128, :])
            ss = small.ap[:, 0:1]
            k.actf(xs, xs.ap, xt, xt.ap, AF.Square, accum=ss, extra_w=[small])
            k.actf(small, small.ap[:, 1:2], small, ss, AF.Sqrt, bias=epsc.ap[:, 0:1], scale=1.0 / D, extra_r=[epsc])
            k.op(DVE, lambda: nc.vector.reciprocal(out=small.ap[:, 2:3], in_=small.ap[:, 1:2]), [small], [small])
            k.ts(DVE, xs, xs.ap, xt, xt.ap, small.ap[:, 2:3], None, ALU.mult, extra_r=[small])
            ia, ib = ab_of_tile(i)
            for hlf in range(2):
                pt = psb_t[hlf]
                for j in range(8):
                    jj = hlf * 8 + j
                    k.tr(pt, psb_ap[:, jj * 128:(jj + 1) * 128], xs, xs.ap[:, jj * 128:(jj + 1) * 128],
                         identb, identb.ap, mark=(j == 7))
                for j in range(8):
                    jj = hlf * 8 + j
                    src = psb_ap[:, jj * 128:(jj + 1) * 128]
                    dst = hT.ap[:, jj, i * 128:(i + 1) * 128]
                    if j % 2 == 0:
                        k.actf(hT, dst, pt, src, AF.Identity, bias=AB.ap[:, ib, jj:jj + 1],
                               scale=AB.ap[:, ia, jj:jj + 1], extra_r=[AB])
                    else:
                        k.ts(DVE, hT, dst, pt, src, AB.ap[:, ia, jj:jj + 1], AB.ap[:, ib, jj:jj + 1],
                             ALU.mult, ALU.add, extra_r=[AB])
        k.free(*xb, *xsb)

    Spp = [[[k.alloc(f"S{h}_{d}_{a}", [128, DV], F32) for a in range(2)] for d in range(2)] for h in range(NH)]
    Scur = [[0, 0] for _ in range(NH)]
    gt_ = dict(lap=k.alloc("g_lap", [128, 1024], F32), Eq=k.alloc("g_Eq", [128, 1024], F32),
               Ek=k.alloc("g_Ek", [128, 1024], F32), kst=k.alloc("g_kst", [128, 8, 128], BF16))
    ss8 = k.sb(es, "ss8", [128, 8], F32)
    rs8 = k.sb(es, "rs8", [128, 8], F32)

    def run(gen):
        for _ in gen:
            pass

    def interleave(main, filler, nfill=1):
        filler = iter(filler) if filler is not None else iter(())
        for _ in main:
            for _i in range(nfill):
                next(filler, None)
        for _ in filler:
            pass

    def chain(*gens):
        for g in gens:
            yield from g

    def gla_chain(h, dirn, n, lr_t, lr_ap_of, kt_t, ktok3, v_t, v_ap_of, order, flag_ap=None, own=None):
        W = n * 128
        hs = slice(h * 128, (h + 1) * 128)
        zb, zap = getn(2 if n > 4 else 1)
        for j in range(n):
            js = slice(j * 128, (j + 1) * 128)
            k.op(PE, lambda: nc.tensor.matmul(zap[:, js], lr_ap_of(j), wgs.ap[:, dirn, hs], start=True, stop=False),
                 [lr_t, wgs], [zb[j // 4]], mark=False)
            k.op(PE, lambda: nc.tensor.matmul(zap[:, js], ones_r.ap[0:1, 0:128], bgs.ap[0:1, dirn, hs], start=False, stop=True),
                 [ones_r, bgs], [zb[j // 4]], mark=(j == n - 1))
        yield
        lap = gt_["lap"]
        k.op(ACT, lambda: nc.scalar.activation(out=lap.ap[:, 0:W], in_=zap[:, 0:W], func=AF.Exp, scale=-1.0), zb, [lap])
        k.op(ACT, lambda: nc.scalar.activation(out=lap.ap[:, 0:W], in_=lap.ap[:, 0:W], func=AF.Ln, bias=1.0), [lap], [lap])
        if flag_ap is not None:
            k.ts(DVE, lap, lap.ap[:, 0:W], lap, lap.ap[:, 0:W], flag_ap, None, ALU.mult, extra_r=[flags])
        yield
        bb, bap = getn(2 if n > 4 else 1)
        kb, kap = getn(2 if n > 4 else 1)
        for j in range(n):
            js = slice(j * 128, (j + 1) * 128)
            k.op(PE, lambda: nc.tensor.matmul(bap[:, js], lap.ap[:, js], tri.ap[:, 2 * dirn, :], start=True, stop=True),
                 [lap, tri], [bb[j // 4]], mark=(j == n - 1))
        for j in range(n):
            js = slice(j * 128, (j + 1) * 128)
            k.op(PE, lambda: nc.tensor.matmul(kap[:, js], tri.ap[:, 2 * dirn + 1, :], lap.ap[:, js], start=True, stop=True),
                 [lap, tri], [kb[j // 4]], mark=(j == n - 1))
        yield
        Eq, Ek, kst = gt_["Eq"], gt_["Ek"], gt_["kst"]
        k.op(ACT, lambda: nc.scalar.activation(out=Eq.ap[:, 0:W], in_=bap[:, 0:W], func=AF.Exp), bb, [Eq])
        k.op(ACT, lambda: nc.scalar.activation(out=lap.ap[:, 0:W], in_=kap[:, 0:W], func=AF.Exp), kb, [lap])
        if own is not None:
            k.op(ACT, lambda: nc.scalar.activation(out=Ek.ap[:, 0:W], in_=bap[:, 0:W], func=AF.Exp, scale=-1.0), bb, [Ek])
            k.tt(DVE, own["qd"], own["qd"].ap[:, dirn, :], own["qT"], own["qT"].ap, Eq, Eq.ap, ALU.mult)
            k.tt(DVE, own["kd"], own["kd"].ap[:, dirn, :], own["kT"], own["kT"].ap, Ek, Ek.ap, ALU.mult)
        es3 = lap.ap[:, 0:W].rearrange("p (j d) -> p j d", d=128)
        if flag_ap is not None:
            k.stt(kst, kst.ap[:, 0:n, :], kt_t, ktok3, flag_ap, lap, es3, ALU.mult, ALU.mult, extra_r=[flags])
        else:
            k.tt(DVE, kst, kst.ap[:, 0:n, :], kt_t, ktok3, lap, es3, ALU.mult)
        yield
        kvb, kvap = getn(4 if n > 2 else 1)
        for j in range(n):
            k.op(PE, lambda: nc.tensor.matmul(kvap[:, j * 256:(j + 1) * 256], kst.ap[:, j, :], v_ap_of(j), start=True, stop=True),
                 [kst, v_t], [kvb[j // 2]], mark=(j == n - 1))
        yield
        first = True
        for idx, j in enumerate(order):
            S_old = Spp[h][dirn][Scur[h][dirn]]
            S_new = Spp[h][dirn][1 - Scur[h][dirn]]
            Scur[h][dirn] = 1 - Scur[h][dirn]
            dec = Eq.ap[:, j * 128 + 127:j * 128 + 128] if dirn == 0 else Eq.ap[:, j * 128:j * 128 + 1]
            kvj = kvap[:, j * 256:(j + 1) * 256]
            if own is not None and first:
                k.copy(ACT, own["Sp"], own["Sp"].ap[:, dirn, j, :], S_old, S_old.ap)
            first = False
            k.stt(S_new, S_new.ap, S_old, S_old.ap, dec, kvb[j // 2], kvj, ALU.mult, ALU.add, extra_r=[Eq])
            if own is not None and idx + 1 < len(order):
                jn = order[idx + 1]
                k.stt(own["Sp"], own["Sp"].ap[:, dirn, jn, :], S_old, S_old.ap, dec, kvb[j // 2], kvj, ALU.mult, ALU.add,
                      extra_r=[Eq])
        yield

    w3 = lambda c0, n: wsrc(w_in, 16, c0, n)

    def proj_lr(hT, ntok, lrT):
        wb, wv = wload(w_in, w3(O_LF, 32), 16, 32)
        for dirn in range(2):
            for t0 in range(0, ntok, 512):
                nt = min(512, ntok - t0)
                b = psr.get()
                for kt in range(16):
                    k.mm(b, b.ap[0:16, 0:nt], wb, wv[:, kt, dirn * 16:(dirn + 1) * 16], hT, hT.ap[:, kt, t0:t0 + nt],
                         kt == 0, kt == 15)
                k.copy(ACT, lrT, lrT.ap[:, dirn, t0:t0 + nt], b, b.ap[0:16, 0:nt])

    def proj_tm_g(hT, tiles, wb, wv, KT, n, consumer):
        for i in tiles:
            b = psr.get()
            for kt in range(KT):
                k.mm(b, b.ap[:, 0:n], hT, hT.ap[:, kt, i * 128:(i + 1) * 128], wb, wv[:, kt, :], kt == 0, kt == KT - 1)
            consumer(i, b)
            yield

    def proj_fm_g(hT, ntok, wb, wv, KT, n, consumer):
        for c in range(n // 128):
            for t0 in range(0, ntok, 512):
                b = psr.get()
                for kt in range(KT):
                    k.mm(b, b.ap[:, 0:512], wb, wv[:, kt, c * 128:(c + 1) * 128], hT, hT.ap[:, kt, t0:t0 + 512],
                         kt == 0, kt == KT - 1)
                consumer(c, t0, b)
                yield

    def proj_tm(hT, tiles, w_t, src3, KT, n, consumer):
        wb, wv = wload(w_t, src3, KT, n)
        run(proj_tm_g(hT, tiles, wb, wv, KT, n, consumer))

    def proj_fm(hT, ntok, w_t, src3, KT, n, consumer):
        wb, wv = wload(w_t, src3, KT, n)
        run(proj_fm_g(hT, ntok, wb, wv, KT, n, consumer))

    for h in range(NH):
        for d in range(2):
            k.op(DVE, lambda S=Spp[h][d][0]: nc.vector.memset(S.ap, 0.0), [], [Spp[h][d][0]])
    hTr = k.alloc("hTr", [128, 16, T_REST], BF16)
    lrTr = k.alloc("lrTr", [16, 2, T_REST], F32)
    build_hT(xr, 10, hTr, lambda i: (0, 1) if i < 8 else (2, 3))
    kvr = [dict(k=k.alloc(f"ktok_r{i}", [128, 10, 128], F32), v=k.alloc(f"vtok_r{i}", [128, 10, 256], BF16)) for i in range(2)]
    X_oth = k.alloc("X_oth", [128, 8, 1024], BF16)
    tap("hTr", hTr, hTr.ap, [128, 16, T_REST], BF16)
    proj_lr(hTr, T_REST, lrTr)

    def p1_proj(h):
        kv = kvr[h % 2]
        wk = wload(w_in, w3(O_K + h * 128, 128), 16, 128)
        wv_ = wload(w_in, w3(O_V + h * 256, 256), 16, 256)
        return chain(
            proj_tm_g(hTr, range(10), wk[0], wk[1], 16, 128,
                      lambda i, b: k.copy(ACT, kv["k"], kv["k"].ap[:, i, :], b, b.ap[:, 0:128])),
            proj_tm_g(hTr, range(10), wv_[0], wv_[1], 16, 256,
                      lambda i, b: k.copy(DVE, kv["v"], kv["v"].ap[:, i, :], b, b.ap[:, 0:256])))

    def p1_xoth():
        gens = []
        for u in range(4):
            w_ = wload(w_in, w3(O_F + u * 256, 256), 16, 256)
            gens.append(proj_tm_g(hTr, range(8), w_[0], w_[1], 16, 256,
                                  lambda i, b, u=u: k.copy(ACT if i % 2 else DVE, X_oth, X_oth.ap[:, i, u * 256:(u + 1) * 256],
                                                           b, b.ap[:, 0:256])))
        return chain(*gens)

    def p1_gla(h):
        kv = kvr[h % 2]
        gens = []
        for dirn in range(2):
            corder = [0, 1] if dirn == 0 else [1, 0]
            oorder = list(range(8)) if dirn == 0 else list(range(7, -1, -1))
            gens.append(gla_chain(h, dirn, 2, lrTr, lambda j, dirn=dirn: lrTr.ap[:, dirn, (8 + j) * 128:(9 + j) * 128],
                                  kv["k"], kv["k"].ap[:, 8:10, :], kv["v"], lambda j: kv["v"].ap[:, 8 + j, :], corder))
            gens.append(gla_chain(h, dirn, 8, lrTr, lambda j, dirn=dirn: lrTr.ap[:, dirn, j * 128:(j + 1) * 128],
                                  kv["k"], kv["k"].ap[:, 0:8, :], kv["v"], lambda j: kv["v"].ap[:, j, :], oorder,
                                  flag_ap=flags.ap[:, dirn:dirn + 1]))
        return chain(*gens)

    run(p1_proj(0))
    for h in range(NH):
        filler = p1_proj(h + 1) if h + 1 < NH else p1_xoth()
        interleave(p1_gla(h), filler, 1)
    for h in range(NH):
        for d in range(2):
            S = Spp[h][d][Scur[h][d]]
            tap(f"S0_{h}_{d}", S, S.ap, [128, DV])
    tap("X_oth", X_oth, X_oth.ap, [128, 8, 1024], BF16)
    X_oth_d = dscr("X_oth_d", [128, 8 * 1024], BF16)
    k.dma(SP, X_oth_d, X_oth_d.ap, X_oth, X_oth.ap.rearrange("p a n -> p (a n)"))
    k.free(hTr, lrTr, X_oth, *[t for d_ in kvr for t in d_.values()])
    if stop_after == "P1":
        return finish()

    build_hT(xo, 8, hTo, lambda i: (0, 1))
    tap("hTo", hTo, hTo.ap, [128, 16, T_OWN], BF16)
    ogT = k.alloc("ogT", [128, 8, T_OWN], BF16)
    lrTo = k.alloc("lrTo", [16, 2, T_OWN], F32)
    hb = [dict(srT=k.alloc(f"srT{i}", [128, 2, T_OWN], BF16), kT=k.alloc(f"kT{i}", [128, T_OWN], F32),
               qT=k.alloc(f"qT{i}", [128, T_OWN], F32), ktk=k.alloc(f"ktk{i}", [128, 8, 128], F32),
               vtk=k.alloc(f"vtk{i}", [128, 8, 256], BF16)) for i in range(2)]
    qd = k.alloc("qd", [128, 2, T_OWN], BF16)
    kd = k.alloc("kd", [128, 2, T_OWN], BF16)
    Sp = k.alloc("Sp", [128, 2, 8, DV], BF16)
    smk = k.alloc("smk", [128, 8 * 256], BF16)
    onb = k.alloc("onb", [128, 8, 256], BF16)
    mask8 = k.alloc("mask8", [128, 8 * 256], BF16)
    k.dma(SP, mask8, mask8.ap, c_mask8, c_mask8.ap)
    proj_lr(hTo, T_OWN, lrTo)

    def p2_proj(h):
        B = hb[h % 2]
        wk = wload(w_in, w3(O_K + h * 128, 128), 16, 128)
        wv_ = wload(w_in, w3(O_V + h * 256, 256), 16, 256)
        wq = wload(w_in, w3(O_Q + h * 128, 128), 16, 128)
        wr = wload(w_in, w3(O_R + h * 256, 256), 16, 256)
        return chain(
            proj_tm_g(hTo, range(8), wk[0], wk[1], 16, 128,
                      lambda i, b: k.copy(ACT, B["ktk"], B["ktk"].ap[:, i, :], b, b.ap[:, 0:128])),
            proj_tm_g(hTo, range(8), wv_[0], wv_[1], 16, 256,
                      lambda i, b: k.copy(DVE, B["vtk"], B["vtk"].ap[:, i, :], b, b.ap[:, 0:256])),
            proj_fm_g(hTo, T_OWN, wk[0], wk[1], 16, 128,
                      lambda c, t0, b: k.copy(ACT, B["kT"], B["kT"].ap[:, t0:t0 + 512], b, b.ap[:, 0:512])),
            proj_fm_g(hTo, T_OWN, wq[0], wq[1], 16, 128,
                      lambda c, t0, b: k.op(ACT, lambda: nc.scalar.mul(out=B["qT"].ap[:, t0:t0 + 512], in_=b.ap[:, 0:512],
                                                                      mul=DK ** -0.5), [b], [B["qT"]])),
            proj_fm_g(hTo, T_OWN, wr[0], wr[1], 16, 256,
                      lambda c, t0, b: k.actf(B["srT"], B["srT"].ap[:, c, t0:t0 + 512], b, b.ap[:, 0:512], AF.Silu)))

    def p2_gla(h):
        B = hb[h % 2]
        own = dict(qT=B["qT"], kT=B["kT"], qd=qd, kd=kd, Sp=Sp)
        for dirn in range(2):
            order = list(range(8)) if dirn == 0 else list(range(7, -1, -1))
            yield from gla_chain(h, dirn, 8, lrTo, lambda j, dirn=dirn: lrTo.ap[:, dirn, j * 128:(j + 1) * 128],
                                 B["ktk"], B["ktk"].ap, B["vtk"], lambda j: B["vtk"].ap[:, j, :], order, own=own)
        scb, scap = getn(4)
        for i in range(8):
            sl = slice(i * 128, (i + 1) * 128)
            for dirn in range(2):
                k.op(PE, lambda: nc.tensor.matmul(scap[:, i * 256 + dirn * 128:i * 256 + (dirn + 1) * 128],
                                                  kd.ap[:, dirn, sl], qd.ap[:, dirn, sl], start=True, stop=True),
                     [kd, qd], [scb[i // 2]], mark=(i == 7 and dirn == 1))
        yield
        k.op(DVE, lambda: nc.vector.tensor_tensor(out=smk.ap, in0=scap, in1=mask8.ap, op=ALU.mult), scb + [mask8], [smk])
        yield
        obb, obap = getn(4)
        for i in range(8):
            sl = slice(i * 128, (i + 1) * 128)
            oa = obap[:, i * 256:(i + 1) * 256]
            wr_ = [obb[i // 2]]
            k.op(PE, lambda: nc.tensor.matmul(oa, smk.ap[:, i * 256:i * 256 + 128], B["vtk"].ap[:, i, :], start=True, stop=False),
                 [smk, B["vtk"]], wr_, mark=False)
            k.op(PE, lambda: nc.tensor.matmul(oa, smk.ap[:, i * 256 + 128:(i + 1) * 256], B["vtk"].ap[:, i, :], start=False, stop=False),
                 [smk, B["vtk"]], wr_, mark=False)
            k.op(PE, lambda: nc.tensor.matmul(oa, qd.ap[:, 0, sl], Sp.ap[:, 0, i, :], start=False, stop=False), [qd, Sp], wr_, mark=False)
            k.op(PE, lambda: nc.tensor.matmul(oa, qd.ap[:, 1, sl], Sp.ap[:, 1, i, :], start=False, stop=True), [qd, Sp], wr_,
                 mark=(i == 7))
        yield
        for i in range(8):
            k.op(ACT, lambda: nc.scalar.activation(out=onb.ap[:, i, :], in_=obap[:, i * 256:(i + 1) * 256], func=AF.Square,
                                                   accum_out=ss8.ap[:, i:i + 1]), [obb[i // 2]], [onb, ss8])
        k.actf(ss8, ss8.ap, ss8, ss8.ap, AF.Sqrt, bias=epsc.ap[:, 0:1], scale=1.0 / DV, extra_r=[epsc])
        k.op(DVE, lambda: nc.vector.reciprocal(out=rs8.ap, in_=ss8.ap), [ss8], [rs8])
        for i in range(8):
            k.op(ACT, lambda: nc.scalar.activation(out=onb.ap[:, i, :], in_=obap[:, i * 256:(i + 1) * 256], func=AF.Copy,
                                                   scale=rs8.ap[:, i:i + 1]), [obb[i // 2], rs8], [onb])
        yield
        for i in range(8):
            for j in range(2):
                c_ = i * 2 + j
                k.tr(psb_t[c_ // 8], psb_ap[:, c_ * 128:(c_ + 1) * 128], onb, onb.ap[:, i, j * 128:(j + 1) * 128],
                     identb, identb.ap, mark=(c_ % 8 == 7))
        yield
        p4 = psb_ap.rearrange("p (i j n) -> p i j n", j=2, n=128)
        for j in range(2):
            c = h * 2 + j
            k.stt(ogT, ogT.ap[:, c, :].rearrange("p (i n) -> p i n", n=128), psb_t[0], p4[:, :, j, :], vecs.ap[:, 32 + c:33 + c],
                  B["srT"], B["srT"].ap[:, j, :].rearrange("p (i n) -> p i n", n=128), ALU.mult, ALU.mult,
                  extra_r=[vecs, psb_t[1]])
        yield

    def p2_xown():
        gens = []
        for u in range(4):
            w_ = wload(w_in, w3(O_F + u * 256, 256), 16, 256)
            gens.append(proj_tm_g(hTo, range(8), w_[0], w_[1], 16, 256,
                                  lambda i, b, u=u: k.copy(ACT if i % 2 else DVE, X_own, X_own.ap[:, i, u * 256:(u + 1) * 256],
                                                           b, b.ap[:, 0:256])))
        return chain(*gens)

    run(p2_proj(0))
    for h in range(NH):
        filler = p2_proj(h + 1) if h + 1 < NH else None
        interleave(p2_gla(h), filler, 2)
    tap("ogT", ogT, ogT.ap, [128, 8, T_OWN], BF16)
    k.free(lrTo, qd, kd, Sp, smk, onb, mask8, *[t for d_ in hb for t in d_.values()], *gt_.values())
    for h in range(NH):
        for d in range(2):
            k.free(*Spp[h][d])
    if stop_after == "GLA":
        return finish()
    fT = k.alloc("fT", [128, 8, T_OWN], BF16)
    X_own = k.alloc("X_own", [128, 8, 1024], BF16)
    X_oth = k.alloc("X_oth2", [128, 8, 1024], BF16)
    run(p2_xown())
    k.dma(SP, X_oth, X_oth.ap.rearrange("p a n -> p (a n)"), X_oth_d, X_oth_d.ap)
    dftc = k.alloc("dftc", [128, 2, 2, 256], BF16)
    ctt = k.alloc("ctt", [128, 16, 256], BF16)
    stt_ = k.alloc("stt", [128, 16, 256], BF16)
    V1 = k.alloc("V1", [128, 8, 256], BF16)
    V2 = k.alloc("V2", [128, 8, 256], BF16)
    k.dma(SP, dftc, dftc.ap, c_dftc, c_dftc.ap)
    for tp in range(4):
        tps = slice(tp * 256, (tp + 1) * 256)
        k.dma(SP, ctt, ctt.ap, c_ct, c_ct.ap.rearrange("(tt p) n -> p tt n", p=128)[:, :, tps])
        k.dma(SP, stt_, stt_.ap, c_st, c_st.ap.rearrange("(tt p) n -> p tt n", p=128)[:, :, tps])
        for cc in range(8):
            for mt, Vt in ((ctt, V1), (stt_, V2)):
                b = psr.get()
                for tt_ in range(16):
                    Xs = X_own if tt_ < 8 else X_oth
                    k.mm(b, b.ap[:, 0:256], Xs, Xs.ap[:, tt_ % 8, cc * 128:(cc + 1) * 128], mt, mt.ap[:, tt_, :],
                         tt_ == 0, tt_ == 15)
                k.copy(ACT if Vt is V1 else DVE, Vt, Vt.ap[:, cc, :], b, b.ap[:, 0:256])
        for g in range(4):
            for j in range(2):
                b = psr.get()
                n = 0
                for s_, Vt in ((0, V1), (1, V2)):
                    for ci in range(2):
                        k.mm(b, b.ap[:, 0:256], dftc, dftc.ap[:, s_, ci, j * 128:(j + 1) * 128], Vt, Vt.ap[:, g * 2 + ci, :],
                             n == 0, n == 3)
                        n += 1
                k.copy(ACT if j else DVE, fT, fT.ap[:, g * 2 + j, tps], b, b.ap[:, 0:256])
    tap("fT", fT, fT.ap, [128, 8, T_OWN], BF16)
    k.free(X_own, dftc, ctt, stt_, V1, V2, X_oth)
    if stop_after == "FNET":
        return finish()
    yT = k.alloc("yT", [128, 16, T_OWN], BF16)
    sga = [k.alloc(f"sga{i}", [128, 512], F32) for i in range(2)]
    sgb = [k.alloc(f"sgb{i}", [128, 512], F32) for i in range(2)]
    it = 0
    for np_ in range(8):
        wa, wav = wload(w_in, w3(O_GA + np_ * 256, 256), 16, 256)
        wb_, wbv = wload(w_in, w3(O_GB + np_ * 256, 256), 16, 256)
        wo = wring.get()
        wov = wo.ap.rearrange("p (a kt n) -> p a kt n", a=2, n=256)
        k.dma(POOL, wo, wov[:, 0], w_gla_out, wsrc(w_gla_out, 8, np_ * 256, 256), join=True)
        k.dma(POOL, wo, wov[:, 1], w_fnet_out, wsrc(w_fnet_out, 8, np_ * 256, 256), join=True)
        for c in range(2):
            n = np_ * 2 + c
            cs_ = slice(c * 128, (c + 1) * 128)
            for t0 in (0, 512):
                ts_ = slice(t0, t0 + 512)
                bga, bgb, bya, byb = psr.get(), psr.get(), psr.get(), psr.get()
                for kt in range(16):
                    k.mm(bga, bga.ap, wa, wav[:, kt, cs_], hTo, hTo.ap[:, kt, ts_], kt == 0, kt == 15)
                for kt in range(16):
                    k.mm(bgb, bgb.ap, wb_, wbv[:, kt, cs_], hTo, hTo.ap[:, kt, ts_], kt == 0, kt == 15)
                for kt in range(8):
                    k.mm(bya, bya.ap, wo, wov[:, 0, kt, cs_], ogT, ogT.ap[:, kt, ts_], kt == 0, kt == 7)
                for kt in range(8):
                    k.mm(byb, byb.ap, wo, wov[:, 1, kt, cs_], fT, fT.ap[:, kt, ts_], kt == 0, kt == 7)
                a_, b_ = sga[it % 2], sgb[it % 2]
                it += 1
                k.actf(a_, a_.ap, bga, bga.ap, AF.Sigmoid)
                k.actf(b_, b_.ap, bgb, bgb.ap, AF.Sigmoid)
                k.tt(DVE, a_, a_.ap, a_, a_.ap, bya, bya.ap, ALU.mult)
                k.tt(DVE, b_, b_.ap, b_, b_.ap, byb, byb.ap, ALU.mult)
                k.tt(DVE, yT, yT.ap[:, n, ts_], a_, a_.ap, b_, b_.ap, ALU.add)
    tap("yT", yT, yT.ap, [128, 16, T_OWN], BF16)
    k.free(ogT, fT, *sga, *sgb)
    if stop_after == "MERGE":
        return finish()

    def resid_proj(actT, KT, w_t, gt_col0, src_t, dst_t):
        gtb = k.alloc("gtb", [128, D], F32)
        k.dma(SP, gtb, gtb.ap, mod_d, mod_d.ap[0:1, gt_col0:gt_col0 + D].partition_broadcast(128))
        xg = [k.alloc(f"xg{i}", [128, 8, 256], F32) for i in range(2)]
        og = [k.alloc(f"og{i}", [128, 8, 256], F32) for i in range(2)]
        wbufs = None
        if KT * 256 > WB:
            wbufs = [k.alloc(f"wd{i}", [128, KT * 256], BF16) for i in range(2)]
        for dg in range(8):
            cs_ = slice(dg * 256, (dg + 1) * 256)
            if wbufs is None:
                wb, wv = wload(w_t, wsrc(w_t, KT, dg * 256, 256), KT, 256)
            else:
                wb = wbufs[dg % 2]
                wv = wb.ap.rearrange("p (kt n) -> p kt n", n=256)
                half = KT // 2
                k.dma(POOL, wb, wv[:, 0:half], w_t, wsrc(w_t, half, dg * 256, 256), join=True)
                k.dma(POOL, wb, wv[:, half:KT], w_t, wsrc(w_t, KT - half, dg * 256, 256, r0=half * 128), join=True)
            x_ = xg[dg % 2]
            o_ = og[dg % 2]
            k.dma(SP, x_, x_.ap, src_t, src_t.ap.rearrange("(i p) n -> p i n", p=128)[:, :, cs_])
            for i in range(8):
                b = psr.get()
                for kt in range(KT):
                    k.mm(b, b.ap[:, 0:256], actT, actT.ap[:, kt, i * 128:(i + 1) * 128], wb, wv[:, kt, :], kt == 0, kt == KT - 1)
                k.tt(DVE, o_, o_.ap[:, i, :], b, b.ap[:, 0:256], gtb, gtb.ap[:, cs_], ALU.mult)
                k.tt(DVE, o_, o_.ap[:, i, :], o_, o_.ap[:, i, :], x_, x_.ap[:, i, :], ALU.add)
            k.dma(SP, dst_t, dst_t.ap.rearrange("(i p) n -> p i n", p=128)[:, :, cs_], o_, o_.ap, join=True)
        k.free(gtb, *xg, *og)
        if wbufs is not None:
            k.free(*wbufs)

    resid_proj(yT, 16, w_out, 2 * D, xo, x1_d)
    k.free(yT)
    build_hT(x1_d, 8, hTo, lambda i: (4, 5))
    tap("h2T", hTo, hTo.ap, [128, 16, T_OWN], BF16)
    if stop_after == "WOUT":
        return finish()

    hid = k.alloc("hid", [128, 44, T_OWN], BF16)
    cvb = [k.alloc(f"cvb{i}", [128, T_OWN], F32) for i in range(2)]
    cgb = [k.alloc(f"cgb{i}", [128, T_OWN], F32) for i in range(2)]
    pairs = [(banks[0], banks[1], ps_h[:, 0:1024]), (banks[2], banks[3], ps_h[:, 1024:2048]),
             (banks[4], banks[5], ps_h[:, 2048:3072]), (banks[6], banks[7], ps_h[:, 3072:4096])]
    pi = 0
    for st in range(22):
        wvl, wvv = wload(w_up, wsrc(w_up, 16, st * 256, 256), 16, 256)
        wgt, wgv = wload(w_up, wsrc(w_up, 16, D_FF + st * 256, 256), 16, 256)
        for c in range(2):
            m = st * 2 + c
            res = []
            for (wt_, wv_, cb_, pcol) in ((wvl, wvv, cvb[m % 2], m), (wgt, wgv, cgb[m % 2], 44 + m)):
                b0, b1, pap = pairs[pi % 4]
                pi += 1
                for hf, bb in ((0, b0), (1, b1)):
                    for kt in range(16):
                        k.mm(bb, pap[:, hf * 512:(hf + 1) * 512], wt_, wv_[:, kt, c * 128:(c + 1) * 128],
                             hTo, hTo.ap[:, kt, hf * 512:(hf + 1) * 512], kt == 0, kt == 15)
                k.op(ACT, lambda: nc.scalar.activation(out=cb_.ap, in_=pap, func=AF.Identity,
                                                       bias=vecs.ap[:, 40 + pcol:41 + pcol],
                                                       scale=cw.ap[:, 1, pcol:pcol + 1]),
                     [b0, b1, vecs, cw], [cb_])
                p3 = pap.rearrange("p (r w) -> p r w", w=64)
                c3 = cb_.ap.rearrange("p (r w) -> p r w", w=64)
                k.op(DVE, lambda: nc.vector.scalar_tensor_tensor(out=c3[:, :, 1:64], in0=p3[:, :, 0:63],
                                                                scalar=cw.ap[:, 0, pcol:pcol + 1], in1=c3[:, :, 1:64],
                                                                op0=ALU.mult, op1=ALU.add),
                     [b0, b1, cw, cb_], [cb_])
                k.op(DVE, lambda: nc.vector.scalar_tensor_tensor(out=c3[:, :, 0:63], in0=p3[:, :, 1:64],
                                                                scalar=cw.ap[:, 2, pcol:pcol + 1], in1=c3[:, :, 0:63],
                                                                op0=ALU.mult, op1=ALU.add),
                     [b0, b1, cw, cb_], [cb_])
                res.append(cb_)
            cv_, cg_ = res
            k.actf(cg_, cg_.ap, cg_, cg_.ap, AF.Silu)
            k.tt(DVE, hid, hid.ap[:, m, :], cg_, cg_.ap, cv_, cv_.ap, ALU.mult)
    tap("hid", hid, hid.ap, [128, 44, T_OWN], BF16)
    k.free(*cvb, *cgb, hTo, *wbufs4)
    if stop_after == "UP":
        return finish()
    resid_proj(hid, 44, w_down, 5 * D, x1_d, x2_d)
    k.free(hid)
    gfb = k.alloc("gfb", [128, D], F32)
    xb = [k.alloc(f"fxb{i}", [128, D], F32) for i in range(2)]
    xq = [k.alloc(f"fxq{i}", [128, D], BF16) for i in range(2)]
    k.dma(SP, gfb, gfb.ap, g_final, g_final.ap[0:1, :].partition_broadcast(128))
    for i in range(8):
        xt = xb[i % 2]
        xs = xq[i % 2]
        k.dma(SP, xt, xt.ap, x2_d, x2_d.ap[i * 128:(i + 1) * 128, :])
        k.actf(xs, xs.ap, xt, xt.ap, AF.Square, accum=small.ap[:, 0:1], extra_w=[small])
        k.actf(small, small.ap[:, 1:2], small, small.ap[:, 0:1], AF.Sqrt, bias=epsc.ap[:, 0:1], scale=1.0 / D, extra_r=[epsc])
        k.op(DVE, lambda: nc.vector.reciprocal(out=small.ap[:, 2:3], in_=small.ap[:, 1:2]), [small], [small])
        k.stt(xt, xt.ap, xt, xt.ap, small.ap[:, 2:3], gfb, gfb.ap, ALU.mult, ALU.mult, extra_r=[small])
        k.dma(SP, out_d, out_d.ap[i * 128:(i + 1) * 128, :], xt, xt.ap, join=True)
    return finish()


def _consts(half):
    c = {}
    c["c_identf"] = np.eye(128, dtype=np.float32)
    c["c_identb"] = np.eye(128).astype(ml_dtypes.bfloat16)
    t = np.arange(128)
    le = (t[:, None] <= t[None, :]).astype(np.float32)
    gt = (t[:, None] > t[None, :]).astype(np.float32)
    ge = (t[:, None] >= t[None, :]).astype(np.float32)
    lt = (t[:, None] < t[None, :]).astype(np.float32)
    c["c_tri"] = np.ascontiguousarray(np.stack([le, gt, ge, lt], axis=1) * np.float32(-1.0 / 16.0)).astype(np.float32)
    c["c_mask"] = np.ascontiguousarray(np.stack([le, ge], axis=1)).astype(np.float32)
    c["c_mask8"] = np.ascontiguousarray(np.tile(np.concatenate([le, ge], axis=1), (1, 8))).astype(ml_dtypes.bfloat16)
    cc = np.arange(256)
    ang = 2 * np.pi * (np.outer(cc, cc) % 256) / 256
    Cc, Sc = np.cos(ang), np.sin(ang)
    dc = np.stack([Cc.reshape(2, 128, 256), (-Sc).reshape(2, 128, 256)], axis=0)
    c["c_dftc"] = np.ascontiguousarray(dc.transpose(2, 0, 1, 3)).astype(ml_dtypes.bfloat16)
    own = np.arange(1024) + half * 1024
    oth = np.arange(1024) + (1 - half) * 1024
    tt = np.concatenate([own, oth])
    ang = 2 * np.pi * (np.outer(tt, own) % 2048) / 2048
    sc = 1.0 / np.sqrt(2048.0 * 256.0)
    c["c_ct"] = (np.cos(ang) * sc).astype(ml_dtypes.bfloat16)
    c["c_st"] = (np.sin(ang) * sc).astype(ml_dtypes.bfloat16)
    fl = np.zeros((128, 2), np.float32)
    fl[:, 0] = 1.0 if half == 1 else 0.0
    fl[:, 1] = 1.0 if half == 0 else 0.0
    c["c_flags"] = fl
    return c


def make_in_maps(x, c, ctx, c_ctx, w_ada, b_ada, g_norm1, w_in, w_gate_f, b_gate_f, w_gate_b, b_gate_b, g_gla,
                 w_gla_out, w_fnet_out, w_out, g_norm2, w_up, conv_w, conv_b, w_down, g_final):
    f = lambda a: np.ascontiguousarray(np.asarray(a, dtype=np.float32))
    x, c, ctx, c_ctx = f(x), f(c), f(ctx), f(c_ctx)
    shared = {
        "w_ada": f(w_ada[0]), "b_ada": f(b_ada[0]).reshape(1, -1), "w_in": f(w_in[0]),
        "wgate": np.ascontiguousarray(np.stack([f(w_gate_f[0]), f(w_gate_b[0])], axis=0)),
        "bgate": np.ascontiguousarray(np.stack([f(b_gate_f[0]), f(b_gate_b[0])], axis=0)),
        "vtab": np.ascontiguousarray(np.concatenate([f(g_norm1[0]).reshape(16, 128), f(g_norm2[0]).reshape(16, 128),
                                                     f(g_gla[0]).reshape(8, 128), f(conv_b[0]).reshape(88, 128)], axis=0)),
        "cwtab": np.ascontiguousarray(f(conv_w[0]).reshape(3, 88, 128)),
        "g_final": f(g_final).reshape(1, -1),
        "w_gla_out": f(w_gla_out[0]), "w_fnet_out": f(w_fnet_out[0]), "w_out": f(w_out[0]),
        "w_up": f(w_up[0]), "w_down": f(w_down[0]),
    }
    consts = [_consts(0), _consts(1)]
    maps = []
    for core in range(8):
        b, half = core // 2, core % 2
        m = dict(shared)
        m.update(consts[half])
        m["xo"] = np.ascontiguousarray(x[b, half * 1024:(half + 1) * 1024])
        m["xr"] = np.ascontiguousarray(np.concatenate([x[b, (1 - half) * 1024:(2 - half) * 1024], ctx[b]], axis=0))
        m["cvec"] = np.ascontiguousarray(np.stack([c[b], c_ctx], axis=0))
        maps.append(m)
    return maps


def kernel(**inputs):
    maps = make_in_maps(**inputs)
    nc = build()
    res = run_bass_kernel_spmd(nc, maps, core_ids=list(range(8)))
    out = np.zeros((4, 2048, 2048), np.float32)
    for core in range(8):
        b, half = core // 2, core % 2
        out[b, half * 1024:(half + 1) * 1024] = res.results[core]["out"]
    return out
```

```python
import bisect
from contextlib import ExitStack

import ml_dtypes
import numpy as np

import concourse.bass as bass
import concourse.mybir as mybir
from concourse.bass_utils import run_bass_kernel_spmd

F32 = mybir.dt.float32
BF16 = mybir.dt.bfloat16
AF = mybir.ActivationFunctionType
ALU = mybir.AluOpType

D = 2048
T_OWN = 1024
T_REST = 1280
NH = 4
DK = 128
DV = 256
D_FF = 5632
EPS = 1e-6
IN_SIZES = (512, 512, 1024, 1024, 16, 16, 1024, 2048, 2048)
OFF = [int(o) for o in np.cumsum((0,) + IN_SIZES)]
O_Q, O_K, O_V, O_R, O_LF, O_LB, O_F, O_GA, O_GB = OFF[:9]
IN_W = OFF[-1]
WB = 4096


class Tok:
    __slots__ = ("eng", "seq", "sem", "val")

    def __init__(self, eng=None, seq=0, sem=None, val=0):
        self.eng, self.seq, self.sem, self.val = eng, seq, sem, val


class T:
    def __init__(self, k, ap, name):
        self.k, self.ap, self.name = k, ap, name
        self.last_w = None
        self.reads = {}
        self.dsem = None
        self.dcnt = 0

    def __getitem__(self, key):
        return self.ap[key]


class Eng:
    def __init__(self, k, name, eng):
        self.k, self.name, self.eng = k, name, eng
        self.sem = k.new_sem("e_" + name)
        self.seq = 0
        self.cnt = 0
        self.mseq = []
        self.mcnt = []
        self.last = None
        self.last_marked = True
        self.waited = {}


class Kern:
    def __init__(self, nc, es):
        self.nc, self.es = nc, es
        self.nsem = 0
        self.pe = Eng(self, "pe", nc.tensor)
        self.act = Eng(self, "act", nc.scalar)
        self.dve = Eng(self, "dve", nc.vector)
        self.pool = Eng(self, "pool", nc.gpsimd)
        self.sp = Eng(self, "sp", nc.sync)
        self.all_t = []

    def new_sem(self, name):
        self.nsem += 1
        return self.es.enter_context(self.nc.semaphore(f"{name}_{self.nsem}"))

    def sb(self, es, name, shape, dt):
        h = es.enter_context(self.nc.sbuf_tensor(name, list(shape), dt))
        return T(self, h[:], name)

    def view(self, ap, name="v"):
        return T(self, ap, name)

    def init_arena(self, nbytes):
        self.arena_h = self.es.enter_context(self.nc.sbuf_tensor("arena", [128, nbytes // 4], F32))
        self.free_list = [(0, nbytes)]
        self.live = {}
        self.dirty = False
        self.bar_sem = self.new_sem("bar")
        self.nbar = 0

    def alloc(self, name, shape, dt):
        if self.dirty:
            self.barrier()
            self.dirty = False
        esz = 2 if dt == BF16 else 4
        per = int(np.prod(shape[1:])) * esz
        nb = (per + 63) // 64 * 64
        for idx, (o, sz) in enumerate(self.free_list):
            if sz >= nb:
                self.free_list[idx] = (o + nb, sz - nb)
                break
        else:
            raise RuntimeError(f"arena OOM for {name} {shape} need {nb} free {self.free_list}")
        ap = self.arena_h[0:shape[0], o // 4:(o + per) // 4]
        if dt == BF16:
            ap = ap.bitcast(BF16)
        if len(shape) > 2:
            names = [f"d{i}" for i in range(len(shape) - 1)]
            kw = {n: s for n, s in zip(names[1:], shape[2:])}
            ap = ap.rearrange("p (" + " ".join(names) + ") -> p " + " ".join(names), **kw)
        t = T(self, ap, name)
        self.live[id(t)] = (o, nb)
        return t

    def free(self, *ts):
        for t in ts:
            o, nb = self.live.pop(id(t))
            self.free_list.append((o, nb))
        fl = sorted(x for x in self.free_list if x[1] > 0)
        out = []
        for o, sz in fl:
            if out and out[-1][0] + out[-1][1] == o:
                out[-1] = (out[-1][0], out[-1][1] + sz)
            else:
                out.append((o, sz))
        self.free_list = out
        self.dirty = True

    def barrier(self):
        comp = [self.pe, self.act, self.dve, self.pool]
        for E in comp:
            if E.last is not None:
                self.mark(E)
        SP = self.sp
        for E in comp:
            if E.cnt > 0 and SP.waited.get(id(E.sem), 0) < E.cnt:
                SP.eng.wait_ge(E.sem, E.cnt)
                SP.waited[id(E.sem)] = E.cnt
        for t in self.all_t:
            if t.dsem is not None and SP.waited.get(id(t.dsem), 0) < t.dcnt:
                SP.eng.wait_ge(t.dsem, t.dcnt)
                SP.waited[id(t.dsem)] = t.dcnt
        SP.eng.sem_inc(self.bar_sem, 1)
        self.nbar += 1
        for E in comp:
            E.eng.wait_ge(self.bar_sem, self.nbar)

    def mark(self, E):
        if E.last_marked:
            return
        E.last.then_inc(E.sem, 1)
        E.cnt += 1
        E.mseq.append(E.seq)
        E.mcnt.append(E.cnt)
        E.last_marked = True

    def resolve(self, tok):
        if tok.sem is not None:
            return tok.sem, tok.val
        E = tok.eng
        i = bisect.bisect_left(E.mseq, tok.seq)
        if i >= len(E.mseq):
            self.mark(E)
            i = len(E.mseq) - 1
        return E.sem, E.mcnt[i]

    def _wait(self, E, tok):
        if tok is None:
            return
        if tok.sem is None and tok.eng is E and E is self.pe:
            return
        sem, val = self.resolve(tok)
        key = id(sem)
        if E.waited.get(key, 0) >= val:
            return
        E.eng.wait_ge(sem, val)
        E.waited[key] = val

    def deps(self, E, reads, writes, join=False):
        for t in reads:
            self._wait(E, t.last_w)
        for t in writes:
            if not (join and t.last_w is not None and t.last_w.sem is not None):
                self._wait(E, t.last_w)
            for tok in t.reads.values():
                self._wait(E, tok)

    def op(self, E, fn, reads=(), writes=(), mark=True):
        self.deps(E, reads, writes)
        ins = fn()
        E.seq += 1
        E.last = ins
        E.last_marked = False
        tok = Tok(E, E.seq)
        if mark:
            self.mark(E)
        for t in writes:
            t.last_w = tok
            t.reads = {}
        for t in reads:
            t.reads[E.name] = tok
        return ins

    def dma(self, Q, dst_t, dst_ap, src_t, src_ap, join=False):
        self.deps(Q, [src_t], [dst_t], join=join)
        if dst_t.dsem is None:
            dst_t.dsem = self.new_sem("d_" + dst_t.name)
            self.all_t.append(dst_t)
        dst_t.dcnt += 16
        Q.eng.dma_start(out=dst_ap, in_=src_ap).then_inc(dst_t.dsem, 16)
        tok = Tok(None, 0, dst_t.dsem, dst_t.dcnt)
        dst_t.last_w = tok
        dst_t.reads = {}
        src_t.reads["dma%d" % id(dst_t.dsem)] = tok

    def mm(self, out_t, out_ap, l_t, l_ap, r_t, r_ap, start, stop, mark=None):
        if mark is None:
            mark = stop
        rd = [l_t, r_t]
        return self.op(self.pe, lambda: self.nc.tensor.matmul(out_ap, l_ap, r_ap, start=start, stop=stop),
                       rd, [out_t], mark=mark)

    def tr(self, out_t, out_ap, in_t, in_ap, id_t, id_ap, mark=True):
        return self.op(self.pe, lambda: self.nc.tensor.transpose(out_ap, in_ap, id_ap), [in_t, id_t], [out_t], mark=mark)

    def actf(self, out_t, out_ap, in_t, in_ap, func, bias=None, scale=None, accum=None, extra_r=(), extra_w=()):
        kw = {}
        if bias is not None:
            kw["bias"] = bias
        if scale is not None:
            kw["scale"] = scale
        if accum is not None:
            kw["accum_out"] = accum
        return self.op(self.act, lambda: self.nc.scalar.activation(out=out_ap, in_=in_ap, func=func, **kw),
                       [in_t] + list(extra_r), [out_t] + list(extra_w))

    def tt(self, E, out_t, out_ap, a_t, a_ap, b_t, b_ap, op):
        return self.op(E, lambda: E.eng.tensor_tensor(out=out_ap, in0=a_ap, in1=b_ap, op=op), [a_t, b_t], [out_t])

    def ts(self, E, out_t, out_ap, a_t, a_ap, s1, s2, op0, op1=None, extra_r=()):
        if op1 is None:
            f = lambda: E.eng.tensor_scalar(out=out_ap, in0=a_ap, scalar1=s1, scalar2=None, op0=op0)
        else:
            f = lambda: E.eng.tensor_scalar(out=out_ap, in0=a_ap, scalar1=s1, scalar2=s2, op0=op0, op1=op1)
        return self.op(E, f, [a_t] + list(extra_r), [out_t])

    def stt(self, out_t, out_ap, a_t, a_ap, sc, b_t, b_ap, op0, op1, extra_r=()):
        return self.op(self.dve, lambda: self.nc.vector.scalar_tensor_tensor(out=out_ap, in0=a_ap, scalar=sc, in1=b_ap,
                                                                            op0=op0, op1=op1),
                       [a_t, b_t] + list(extra_r), [out_t])

    def copy(self, E, out_t, out_ap, in_t, in_ap):
        if E is self.act:
            return self.op(E, lambda: self.nc.scalar.copy(out=out_ap, in_=in_ap), [in_t], [out_t])
        return self.op(E, lambda: E.eng.tensor_copy(out=out_ap, in_=in_ap), [in_t], [out_t])


class Ring:
    def __init__(self, items):
        self.items, self.i = items, 0

    def get(self):
        t = self.items[self.i % len(self.items)]
        self.i += 1
        return t


def build(taps=None, stop_after=None):
    taps = taps or []
    nc = bass.Bass("TRN2", target_bir_lowering=False)
    es = ExitStack()
    k = Kern(nc, es)
    PE, ACT, DVE, POOL, SP = k.pe, k.act, k.dve, k.pool, k.sp

    def din(name, shape, dt=F32):
        return T(k, nc.dram_tensor(name, list(shape), dt, kind="ExternalInput").ap(), name)

    def dscr(name, shape, dt=F32):
        return T(k, nc.dram_tensor(name, list(shape), dt, kind="Internal").ap(), name)

    xo = din("xo", [T_OWN, D])
    xr = din("xr", [T_REST, D])
    cvec = din("cvec", [2, D])
    w_ada = din("w_ada", [D, 6 * D])
    b_ada = din("b_ada", [1, 6 * D])
    w_in = din("w_in", [D, IN_W])
    wgate = din("wgate", [2, 16, 512])
    bgate = din("bgate", [2, 512])
    vtab = din("vtab", [128, 128])
    cwtab = din("cwtab", [3, 88, 128])
    g_final = din("g_final", [1, D])
    w_gla_out = din("w_gla_out", [1024, D])
    w_fnet_out = din("w_fnet_out", [1024, D])
    w_out = din("w_out", [D, D])
    w_up = din("w_up", [D, 2 * D_FF])
    w_down = din("w_down", [D_FF, D])
    c_identf = din("c_identf", [128, 128])
    c_identb = din("c_identb", [128, 128], BF16)
    c_tri = din("c_tri", [128, 4, 128])
    c_mask = din("c_mask", [128, 2, 128])
    c_mask8 = din("c_mask8", [128, 8 * 256], BF16)
    c_dftc = din("c_dftc", [128, 2, 2, 256], BF16)
    c_ct = din("c_ct", [2048, 1024], BF16)
    c_st = din("c_st", [2048, 1024], BF16)
    c_flags = din("c_flags", [128, 2])
    out_d = T(k, nc.dram_tensor("out", [T_OWN, D], F32, kind="ExternalOutput").ap(), "out")
    mod_d = dscr("mod_d", [2, 6 * D])
    x1_d = dscr("x1_d", [T_OWN, D])
    x2_d = dscr("x2_d", [T_OWN, D])
    tap_outs = []

    def tap(name, t, ap, shape, dt=F32):
        if name not in taps:
            return
        o = T(k, nc.dram_tensor("tap_" + name, list(shape), dt, kind="ExternalOutput").ap(), "tap_" + name)
        k.dma(SP, o, o.ap, t, ap)
        tap_outs.append(o)

    def finish():
        for o in [out_d] + tap_outs:
            if o.dsem is not None:
                SP.eng.wait_ge(o.dsem, o.dcnt)
        es.close()
        return nc

    ps_h = es.enter_context(nc.psum_tensor("ps", [128, 4096], F32))
    banks = [T(k, ps_h[:, i * 512:(i + 1) * 512], f"bank{i}") for i in range(8)]
    bank_ptr = [0]

    def getn(n):
        p = (bank_ptr[0] + n - 1) // n * n
        if p + n > 8:
            p = 0
        bank_ptr[0] = p + n
        return banks[p:p + n], ps_h[:, p * 512:(p + n) * 512]

    class _PSR:
        def get(self):
            return getn(1)[0][0]
    psr = _PSR()
    psb_t = [banks[6], banks[7]]
    psb_ap = ps_h[:, 3072:4096].bitcast(BF16)

    identf = k.sb(es, "identf", [128, 128], F32)
    identb = k.sb(es, "identb", [128, 128], BF16)
    tri = k.sb(es, "tri", [128, 4, 128], F32)
    mask = k.sb(es, "mask", [128, 2, 128], F32)
    flags = k.sb(es, "flags", [128, 2], F32)
    ones_r = k.sb(es, "ones_r", [1, 128], F32)
    epsc = k.sb(es, "epsc", [128, 1], F32)
    vecs = k.sb(es, "vecs", [128, 128], F32)
    cw = k.sb(es, "cw", [128, 3, 88], F32)
    modT = k.sb(es, "modT", [128, 96], F32)
    cmodT = k.sb(es, "cmodT", [128, 32], F32)
    AB = k.sb(es, "AB", [128, 6, 16], F32)
    wgs = k.sb(es, "wgs", [16, 2, 512], F32)
    bgs = k.sb(es, "bgs", [1, 2, 512], F32)
    small = k.sb(es, "small", [128, 8], F32)
    sm_slots = [k.sb(es, f"sm{i}", [128, 4], F32) for i in range(2)]
    tb = k.sb(es, "tb", [128, 128], F32)
    csb = k.sb(es, "csb", [128, 2, 16], BF16)
    a2st = [k.sb(es, f"a2st{i}", [1, 256], F32) for i in range(2)]
    btmp = k.sb(es, "btmp", [128, 64], F32)
    k.init_arena(190 * 1024)
    wbufs4 = [k.alloc(f"wb{i}", [128, WB], BF16) for i in range(4)]
    wring = Ring(wbufs4)
    hTo = k.alloc("hTo", [128, 16, T_OWN], BF16)

    for t, d in ((identf, c_identf), (identb, c_identb), (tri, c_tri), (mask, c_mask), (flags, c_flags)):
        k.dma(SP, t, t.ap, d, d.ap)
    k.dma(SP, wgs, wgs.ap, wgate, wgate.ap.rearrange("a r n -> r a n"))
    k.dma(SP, bgs, bgs.ap, bgate, bgate.ap.rearrange("(o a) n -> o a n", o=1))
    k.op(DVE, lambda: nc.vector.memset(ones_r.ap, 1.0), [], [ones_r])
    k.op(DVE, lambda: nc.vector.memset(epsc.ap, EPS), [], [epsc])

    def wload(src_t, src3, KT, n):
        b = wring.get()
        v = b.ap[:, 0:KT * n].rearrange("p (kt n) -> p kt n", n=n)
        k.dma(POOL, b, v, src_t, src3)
        return b, v

    def wsrc(w_t, KT, c0, n, r0=0):
        return w_t.ap[r0:r0 + KT * 128, :].rearrange("(kt p) n -> p kt n", p=128)[:, :, c0:c0 + n]

    def load_T(src_t, src_ap, rows, dst_t, dst_ap):
        k.dma(SP, tb, tb.ap[0:rows, :], src_t, src_ap)
        b = psr.get()
        k.tr(b, b.ap[:, 0:rows], tb, tb.ap[0:rows, :], identf, identf.ap[0:rows, 0:rows])
        k.copy(DVE, dst_t, dst_ap, b, b.ap[:, 0:rows])

    load_T(vtab, vtab.ap, 128, vecs, vecs.ap)
    for j in range(3):
        load_T(cwtab, cwtab.ap[j], 88, cw, cw.ap[:, j, :])

    cs = k.alloc("cs", [128, 2, 16], F32)
    barow = k.alloc("barow", [1, 2 * D], F32)
    mrow = k.alloc("mrow", [1, 2 * D], F32)
    crow = k.alloc("crow", [1, 2 * D], F32)
    for r in range(2):
        k.dma(SP, cs, cs.ap[:, r, :], cvec, cvec.ap[r].rearrange("(p k) -> p k", k=16), join=True)
    k.dma(SP, barow, barow.ap, b_ada, b_ada.ap[0:1, 0:2 * D])
    k.actf(csb, csb.ap, cs, cs.ap, AF.Silu)
    wa3 = w_ada.ap.rearrange("(p k) n -> p k n", k=16)
    for g in range(16):
        wb, wv = wload(w_ada, wa3[:, :, g * 256:(g + 1) * 256], 16, 256)
        for r in range(2):
            b = psr.get()
            for kk in range(16):
                k.mm(b, b.ap[0:1, 0:256], csb, csb.ap[:, r, kk:kk + 1], wb, wv[:, kk, :], kk == 0, kk == 15)
            dst = mrow if r == 0 else crow
            k.tt(DVE, dst, dst.ap[0:1, g * 256:(g + 1) * 256], b, b.ap[0:1, 0:256],
                 barow, barow.ap[0:1, g * 256:(g + 1) * 256], ALU.add)
    k.dma(SP, mod_d, mod_d.ap[0:1, 0:2 * D], mrow, mrow.ap, join=True)
    k.dma(SP, mod_d, mod_d.ap[1:2, 0:2 * D], crow, crow.ap, join=True)
    load_T(mod_d, mod_d.ap[0, 0:2 * D].rearrange("(j p) -> j p", p=128), 32, modT, modT.ap[:, 0:32])
    load_T(mod_d, mod_d.ap[1, 0:2 * D].rearrange("(j p) -> j p", p=128), 32, cmodT, cmodT.ap)

    def mkAB(ia, ib, gcol, m_t, sh0, sc0):
        k.ts(DVE, AB, AB.ap[:, ia, :], m_t, m_t.ap[:, sc0:sc0 + 16], 1.0, None, ALU.add)
        k.tt(DVE, AB, AB.ap[:, ia, :], AB, AB.ap[:, ia, :], vecs, vecs.ap[:, gcol:gcol + 16], ALU.mult)
        k.copy(DVE, AB, AB.ap[:, ib, :], m_t, m_t.ap[:, sh0:sh0 + 16])
    mkAB(0, 1, 0, modT, 0, 16)
    mkAB(2, 3, 0, cmodT, 0, 16)
    k.free(cs, barow, mrow, crow)

    mod_slots = [T(k, mod_d.ap, f"mod_slot{i}") for i in range(2)]

    def a2_gen():
        for g in range(16, 48):
            wb, wv = wload(w_ada, wa3[:, :, g * 256:(g + 1) * 256], 16, 256)
            b = psr.get()
            for kk in range(16):
                k.mm(b, b.ap[0:1, 0:256], csb, csb.ap[:, 0, kk:kk + 1], wb, wv[:, kk, :], kk == 0, kk == 15)
            st = a2st[g % 2]
            md = mod_slots[g % 2]
            k.copy(DVE, st, st.ap, b, b.ap[0:1, 0:256])
            k.dma(SP, md, mod_d.ap[0:1, g * 256:(g + 1) * 256], st, st.ap)
            yield
        for md in mod_slots:
            k._wait(SP, md.last_w)
        load_T(mod_d, mod_d.ap[0, 2 * D:6 * D].rearrange("(j p) -> j p", p=128), 64, modT, modT.ap[:, 32:96])
        load_T(b_ada, b_ada.ap[0, 2 * D:6 * D].rearrange("(j p) -> j p", p=128), 64, btmp, btmp.ap)
        k.tt(DVE, modT, modT.ap[:, 32:96], modT, modT.ap[:, 32:96], btmp, btmp.ap, ALU.add)
        mkAB(4, 5, 16, modT, 48, 64)
        yield
    a2 = a2_gen()
    tap("modT", modT, modT.ap, [128, 96])
    tap("AB", AB, AB.ap, [128, 6, 16])
    if stop_after == "A":
        return finish()

    def build_hT_g(src_t, ntiles, hT, ab_of_tile, nbuf=2):
        xb = [k.alloc(f"xbuf{i}", [128, D], F32) for i in range(nbuf)]
        xsb = [k.alloc(f"xsb{i}", [128, D], BF16) for i in range(nbuf)]

        def stage_a(i):
            xt = xb[i % nbuf]
            xs = xsb[i % nbuf]
            sm = sm_slots[i % 2]
            k.dma(SP, xt, xt.ap, src_t, src_t.ap[i * 128:(i + 1) * 128, :])
            ss = sm.ap[:, 0:1]
            k.actf(xs, xs.ap, xt, xt.ap, AF.Square, accum=ss, extra_w=[sm])
            k.actf(sm, sm.ap[:, 1:2], sm, ss, AF.Sqrt, bias=epsc.ap[:, 0:1], scale=1.0 / D, extra_r=[epsc])
            k.op(DVE, lambda: nc.vector.reciprocal(out=sm.ap[:, 2:3], in_=sm.ap[:, 1:2]), [sm], [sm])
            k.ts(DVE, xs, xs.ap, xt, xt.ap, sm.ap[:, 2:3], None, ALU.mult, extra_r=[sm])

        def stage_b(i):
            xs = xsb[i % nbuf]
            ia, ib = ab_of_tile(i)
            for hlf in range(2):
                pt = psb_t[hlf]
                for j in range(8):
                    jj = hlf * 8 + j
                    k.tr(pt, psb_ap[:, jj * 128:(jj + 1) * 128], xs, xs.ap[:, jj * 128:(jj + 1) * 128],
                         identb, identb.ap, mark=(j == 7))
                for j in range(8):
                    jj = hlf * 8 + j
                    src = psb_ap[:, jj * 128:(jj + 1) * 128]
                    dst = hT.ap[:, jj, i * 128:(i + 1) * 128]
                    if j % 2 == 0:
                        k.actf(hT, dst, pt, src, AF.Identity, bias=AB.ap[:, ib, jj:jj + 1],
                               scale=AB.ap[:, ia, jj:jj + 1], extra_r=[AB])
                    else:
                        k.ts(DVE, hT, dst, pt, src, AB.ap[:, ia, jj:jj + 1], AB.ap[:, ib, jj:jj + 1],
                             ALU.mult, ALU.add, extra_r=[AB])

        if nbuf > 1:
            stage_a(0)
        for i in range(ntiles):
            if nbuf > 1:
                if i + 1 < ntiles:
                    stage_a(i + 1)
            else:
                stage_a(i)
            stage_b(i)
            yield
        k.free(*xb, *xsb)

    def build_hT(src_t, ntiles, hT, ab_of_tile):
        for _ in build_hT_g(src_t, ntiles, hT, ab_of_tile):
            pass

    Spp = [[[k.alloc(f"S{h}_{d}_{a}", [128, DV], F32) for a in range(2)] for d in range(2)] for h in range(NH)]
    Scur = [[0, 0] for _ in range(NH)]
    gt_ = dict(lap=k.alloc("g_lap", [128, 1024], F32), Eq=k.alloc("g_Eq", [128, 1024], F32),
               kst=k.alloc("g_kst", [128, 8, 128], BF16))
    ss8 = k.sb(es, "ss8", [128, 8], F32)
    rs8 = k.sb(es, "rs8", [128, 8], F32)

    def run(gen):
        for _ in gen:
            pass

    def interleave(main, filler, nfill=1):
        filler = iter(filler) if filler is not None else iter(())
        for _ in main:
            for _i in range(nfill):
                next(filler, None)
        for _ in filler:
            pass

    def chain(*gens):
        for g in gens:
            yield from g

    def gla_chain(h, dirn, n, lr_t, lr_ap_of, kt_t, ktok3, v_t, v_ap_of, order, flag_ap=None, own=None):
        W = n * 128
        hs = slice(h * 128, (h + 1) * 128)
        zb, zap = getn(2 if n > 4 else 1)
        for j in range(n):
            js = slice(j * 128, (j + 1) * 128)
            k.op(PE, lambda: nc.tensor.matmul(zap[:, js], lr_ap_of(j), wgs.ap[:, dirn, hs], start=True, stop=False),
                 [lr_t, wgs], [zb[j // 4]], mark=False)
            k.op(PE, lambda: nc.tensor.matmul(zap[:, js], ones_r.ap[0:1, 0:128], bgs.ap[0:1, dirn, hs], start=False, stop=True),
                 [ones_r, bgs], [zb[j // 4]], mark=(j == n - 1))
        yield
        lap = gt_["lap"]
        k.op(ACT, lambda: nc.scalar.activation(out=lap.ap[:, 0:W], in_=zap[:, 0:W], func=AF.Exp, scale=-1.0), zb, [lap])
        k.op(ACT, lambda: nc.scalar.activation(out=lap.ap[:, 0:W], in_=lap.ap[:, 0:W], func=AF.Ln, bias=1.0), [lap], [lap])
        if flag_ap is not None:
            k.ts(DVE, lap, lap.ap[:, 0:W], lap, lap.ap[:, 0:W], flag_ap, None, ALU.mult, extra_r=[flags])
        yield
        bb, bap = getn(2 if n > 4 else 1)
        kb, kap = getn(2 if n > 4 else 1)
        for j in range(n):
            js = slice(j * 128, (j + 1) * 128)
            k.op(PE, lambda: nc.tensor.matmul(bap[:, js], lap.ap[:, js], tri.ap[:, 2 * dirn, :], start=True, stop=True),
                 [lap, tri], [bb[j // 4]], mark=(j == n - 1))
        for j in range(n):
            js = slice(j * 128, (j + 1) * 128)
            k.op(PE, lambda: nc.tensor.matmul(kap[:, js], tri.ap[:, 2 * dirn + 1, :], lap.ap[:, js], start=True, stop=True),
                 [lap, tri], [kb[j // 4]], mark=(j == n - 1))
        yield
        Eq, Ek, kst = gt_["Eq"], gt_.get("Ek"), gt_["kst"]
        k.op(ACT, lambda: nc.scalar.activation(out=Eq.ap[:, 0:W], in_=bap[:, 0:W], func=AF.Exp), bb, [Eq])
        k.op(ACT, lambda: nc.scalar.activation(out=lap.ap[:, 0:W], in_=kap[:, 0:W], func=AF.Exp), kb, [lap])
        if own is not None:
            k.op(ACT, lambda: nc.scalar.activation(out=Ek.ap[:, 0:W], in_=bap[:, 0:W], func=AF.Exp, scale=-1.0), bb, [Ek])
            k.tt(DVE, own["qd"], own["qd"].ap[:, dirn, :], own["qT"], own["qT"].ap, Eq, Eq.ap, ALU.mult)
            k.tt(DVE, own["kd"], own["kd"].ap[:, dirn, :], own["kT"], own["kT"].ap, Ek, Ek.ap, ALU.mult)
        es3 = lap.ap[:, 0:W].rearrange("p (j d) -> p j d", d=128)
        if flag_ap is not None:
            k.stt(kst, kst.ap[:, 0:n, :], kt_t, ktok3, flag_ap, lap, es3, ALU.mult, ALU.mult, extra_r=[flags])
        else:
            k.tt(DVE, kst, kst.ap[:, 0:n, :], kt_t, ktok3, lap, es3, ALU.mult)
        yield
        kvb, kvap = getn(4 if n > 2 else 1)
        for j in range(n):
            k.op(PE, lambda: nc.tensor.matmul(kvap[:, j * 256:(j + 1) * 256], kst.ap[:, j, :], v_ap_of(j), start=True, stop=True),
                 [kst, v_t], [kvb[j // 2]], mark=(j == n - 1))
        yield
        first = True
        for idx, j in enumerate(order):
            S_old = Spp[h][dirn][Scur[h][dirn]]
            S_new = Spp[h][dirn][1 - Scur[h][dirn]]
            Scur[h][dirn] = 1 - Scur[h][dirn]
            dec = Eq.ap[:, j * 128 + 127:j * 128 + 128] if dirn == 0 else Eq.ap[:, j * 128:j * 128 + 1]
            kvj = kvap[:, j * 256:(j + 1) * 256]
            if own is not None and first:
                k.copy(ACT, own["Sp"], own["Sp"].ap[:, dirn, j, :], S_old, S_old.ap)
            first = False
            k.stt(S_new, S_new.ap, S_old, S_old.ap, dec, kvb[j // 2], kvj, ALU.mult, ALU.add, extra_r=[Eq])
            if own is not None and idx + 1 < len(order):
                jn = order[idx + 1]
                k.stt(own["Sp"], own["Sp"].ap[:, dirn, jn, :], S_old, S_old.ap, dec, kvb[j // 2], kvj, ALU.mult, ALU.add,
                      extra_r=[Eq])
        yield

    w3 = lambda c0, n: wsrc(w_in, 16, c0, n)

    def proj_lr(hT, ntok, lrT):
        wb, wv = wload(w_in, w3(O_LF, 32), 16, 32)
        for dirn in range(2):
            for t0 in range(0, ntok, 512):
                nt = min(512, ntok - t0)
                b = psr.get()
                for kt in range(16):
                    k.mm(b, b.ap[0:16, 0:nt], wb, wv[:, kt, dirn * 16:(dirn + 1) * 16], hT, hT.ap[:, kt, t0:t0 + nt],
                         kt == 0, kt == 15)
                k.copy(ACT, lrT, lrT.ap[:, dirn, t0:t0 + nt], b, b.ap[0:16, 0:nt])

    def proj_tm_g(hT, tiles, wb, wv, KT, n, consumer):
        for i in tiles:
            b = psr.get()
            for kt in range(KT):
                k.mm(b, b.ap[:, 0:n], hT, hT.ap[:, kt, i * 128:(i + 1) * 128], wb, wv[:, kt, :], kt == 0, kt == KT - 1)
            consumer(i, b)
            yield

    def proj_fm_g(hT, ntok, wb, wv, KT, n, consumer):
        for c in range(n // 128):
            for t0 in range(0, ntok, 512):
                b = psr.get()
                for kt in range(KT):
                    k.mm(b, b.ap[:, 0:512], wb, wv[:, kt, c * 128:(c + 1) * 128], hT, hT.ap[:, kt, t0:t0 + 512],
                         kt == 0, kt == KT - 1)
                consumer(c, t0, b)
                yield

    def proj_tm(hT, tiles, w_t, src3, KT, n, consumer):
        wb, wv = wload(w_t, src3, KT, n)
        run(proj_tm_g(hT, tiles, wb, wv, KT, n, consumer))

    def proj_fm(hT, ntok, w_t, src3, KT, n, consumer):
        wb, wv = wload(w_t, src3, KT, n)
        run(proj_fm_g(hT, ntok, wb, wv, KT, n, consumer))

    for h in range(NH):
        for d in range(2):
            k.op(DVE, lambda S=Spp[h][d][0]: nc.vector.memset(S.ap, 0.0), [], [Spp[h][d][0]])
    hTr = k.alloc("hTr", [128, 16, T_REST], BF16)
    lrTr = k.alloc("lrTr", [16, 2, T_REST], F32)
    build_hT(xr, 10, hTr, lambda i: (0, 1) if i < 8 else (2, 3))
    kvr = [dict(k=k.alloc(f"ktok_r{i}", [128, 10, 128], F32), v=k.alloc(f"vtok_r{i}", [128, 10, 256], BF16)) for i in range(2)]
    X_oth = k.alloc("X_oth", [128, 8, 1024], BF16)
    tap("hTr", hTr, hTr.ap, [128, 16, T_REST], BF16)
    proj_lr(hTr, T_REST, lrTr)

    def p1_proj(h):
        kv = kvr[h % 2]
        wk = wload(w_in, w3(O_K + h * 128, 128), 16, 128)
        wv_ = wload(w_in, w3(O_V + h * 256, 256), 16, 256)
        return chain(
            proj_tm_g(hTr, range(10), wk[0], wk[1], 16, 128,
                      lambda i, b: k.copy(ACT, kv["k"], kv["k"].ap[:, i, :], b, b.ap[:, 0:128])),
            proj_tm_g(hTr, range(10), wv_[0], wv_[1], 16, 256,
                      lambda i, b: k.copy(DVE, kv["v"], kv["v"].ap[:, i, :], b, b.ap[:, 0:256])))

    def p1_xoth():
        gens = []
        for u in range(4):
            w_ = wload(w_in, w3(O_F + u * 256, 256), 16, 256)
            gens.append(proj_tm_g(hTr, range(8), w_[0], w_[1], 16, 256,
                                  lambda i, b, u=u: k.copy(ACT if i % 2 else DVE, X_oth, X_oth.ap[:, i, u * 256:(u + 1) * 256],
                                                           b, b.ap[:, 0:256])))
        return chain(*gens)

    def p1_gla(h):
        kv = kvr[h % 2]
        gens = []
        for dirn in range(2):
            corder = [0, 1] if dirn == 0 else [1, 0]
            oorder = list(range(8)) if dirn == 0 else list(range(7, -1, -1))
            gens.append(gla_chain(h, dirn, 2, lrTr, lambda j, dirn=dirn: lrTr.ap[:, dirn, (8 + j) * 128:(9 + j) * 128],
                                  kv["k"], kv["k"].ap[:, 8:10, :], kv["v"], lambda j: kv["v"].ap[:, 8 + j, :], corder))
            gens.append(gla_chain(h, dirn, 8, lrTr, lambda j, dirn=dirn: lrTr.ap[:, dirn, j * 128:(j + 1) * 128],
                                  kv["k"], kv["k"].ap[:, 0:8, :], kv["v"], lambda j: kv["v"].ap[:, j, :], oorder,
                                  flag_ap=flags.ap[:, dirn:dirn + 1]))
        return chain(*gens)

    import itertools
    run(p1_proj(0))
    hto_g = build_hT_g(xo, 8, hTo, lambda i: (0, 1), nbuf=1)
    for h in range(NH):
        if h + 1 < NH:
            filler = chain(p1_proj(h + 1), itertools.islice(hto_g, 2), itertools.islice(a2, 8))
        else:
            filler = chain(itertools.islice(hto_g, 1), p1_xoth(), a2, hto_g)
        interleave(p1_gla(h), filler, 2 if h + 1 == NH else 1)
    for h in range(NH):
        for d in range(2):
            S = Spp[h][d][Scur[h][d]]
            tap(f"S0_{h}_{d}", S, S.ap, [128, DV])
    tap("X_oth", X_oth, X_oth.ap, [128, 8, 1024], BF16)
    X_oth_d = dscr("X_oth_d", [128, 8 * 1024], BF16)
    k.dma(SP, X_oth_d, X_oth_d.ap, X_oth, X_oth.ap.rearrange("p a n -> p (a n)"))
    k.free(hTr, lrTr, X_oth, *[t for d_ in kvr for t in d_.values()])
    if stop_after == "P1":
        return finish()

    tap("hTo", hTo, hTo.ap, [128, 16, T_OWN], BF16)
    gt_["Ek"] = k.alloc("g_Ek", [128, 1024], F32)
    ogT = k.alloc("ogT", [128, 8, T_OWN], BF16)
    lrTo = k.alloc("lrTo", [16, 2, T_OWN], F32)
    hb = [dict(srT=k.alloc(f"srT{i}", [128, 2, T_OWN], BF16), kT=k.alloc(f"kT{i}", [128, T_OWN], F32),
               qT=k.alloc(f"qT{i}", [128, T_OWN], F32), ktk=k.alloc(f"ktk{i}", [128, 8, 128], F32),
               vtk=k.alloc(f"vtk{i}", [128, 8, 256], BF16)) for i in range(2)]
    qd = k.alloc("qd", [128, 2, T_OWN], BF16)
    kd = k.alloc("kd", [128, 2, T_OWN], BF16)
    Sp = k.alloc("Sp", [128, 2, 8, DV], BF16)
    smk = k.alloc("smk", [128, 8 * 256], BF16)
    onb = k.alloc("onb", [128, 8, 256], BF16)
    mask8 = k.alloc("mask8", [128, 8 * 256], BF16)
    k.dma(SP, mask8, mask8.ap, c_mask8, c_mask8.ap)
    proj_lr(hTo, T_OWN, lrTo)

    def p2_proj(h):
        B = hb[h % 2]
        wk = wload(w_in, w3(O_K + h * 128, 128), 16, 128)
        wv_ = wload(w_in, w3(O_V + h * 256, 256), 16, 256)
        wq = wload(w_in, w3(O_Q + h * 128, 128), 16, 128)
        wr = wload(w_in, w3(O_R + h * 256, 256), 16, 256)
        return chain(
            proj_tm_g(hTo, range(8), wk[0], wk[1], 16, 128,
                      lambda i, b: k.copy(ACT, B["ktk"], B["ktk"].ap[:, i, :], b, b.ap[:, 0:128])),
            proj_tm_g(hTo, range(8), wv_[0], wv_[1], 16, 256,
                      lambda i, b: k.copy(DVE, B["vtk"], B["vtk"].ap[:, i, :], b, b.ap[:, 0:256])),
            proj_fm_g(hTo, T_OWN, wk[0], wk[1], 16, 128,
                      lambda c, t0, b: k.copy(ACT, B["kT"], B["kT"].ap[:, t0:t0 + 512], b, b.ap[:, 0:512])),
            proj_fm_g(hTo, T_OWN, wq[0], wq[1], 16, 128,
                      lambda c, t0, b: k.op(ACT, lambda: nc.scalar.mul(out=B["qT"].ap[:, t0:t0 + 512], in_=b.ap[:, 0:512],
                                                                      mul=DK ** -0.5), [b], [B["qT"]])),
            proj_fm_g(hTo, T_OWN, wr[0], wr[1], 16, 256,
                      lambda c, t0, b: k.actf(B["srT"], B["srT"].ap[:, c, t0:t0 + 512], b, b.ap[:, 0:512], AF.Silu)))

    def p2_gla(h):
        B = hb[h % 2]
        own = dict(qT=B["qT"], kT=B["kT"], qd=qd, kd=kd, Sp=Sp)
        for dirn in range(2):
            order = list(range(8)) if dirn == 0 else list(range(7, -1, -1))
            yield from gla_chain(h, dirn, 8, lrTo, lambda j, dirn=dirn: lrTo.ap[:, dirn, j * 128:(j + 1) * 128],
                                 B["ktk"], B["ktk"].ap, B["vtk"], lambda j: B["vtk"].ap[:, j, :], order, own=own)
        scb, scap = getn(4)
        for i in range(8):
            sl = slice(i * 128, (i + 1) * 128)
            for dirn in range(2):
                k.op(PE, lambda: nc.tensor.matmul(scap[:, i * 256 + dirn * 128:i * 256 + (dirn + 1) * 128],
                                                  kd.ap[:, dirn, sl], qd.ap[:, dirn, sl], start=True, stop=True),
                     [kd, qd], [scb[i // 2]], mark=(i == 7 and dirn == 1))
        yield
        k.op(DVE, lambda: nc.vector.tensor_tensor(out=smk.ap, in0=scap, in1=mask8.ap, op=ALU.mult), scb + [mask8], [smk])
        yield
        obb, obap = getn(4)
        for i in range(8):
            sl = slice(i * 128, (i + 1) * 128)
            oa = obap[:, i * 256:(i + 1) * 256]
            wr_ = [obb[i // 2]]
            k.op(PE, lambda: nc.tensor.matmul(oa, smk.ap[:, i * 256:i * 256 + 128], B["vtk"].ap[:, i, :], start=True, stop=False),
                 [smk, B["vtk"]], wr_, mark=False)
            k.op(PE, lambda: nc.tensor.matmul(oa, smk.ap[:, i * 256 + 128:(i + 1) * 256], B["vtk"].ap[:, i, :], start=False, stop=False),
                 [smk, B["vtk"]], wr_, mark=False)
            k.op(PE, lambda: nc.tensor.matmul(oa, qd.ap[:, 0, sl], Sp.ap[:, 0, i, :], start=False, stop=False), [qd, Sp], wr_, mark=False)
            k.op(PE, lambda: nc.tensor.matmul(oa, qd.ap[:, 1, sl], Sp.ap[:, 1, i, :], start=False, stop=True), [qd, Sp], wr_,
                 mark=(i == 7))
        yield
        for i in range(8):
            k.op(ACT, lambda: nc.scalar.activation(out=onb.ap[:, i, :], in_=obap[:, i * 256:(i + 1) * 256], func=AF.Square,
                                                   accum_out=ss8.ap[:, i:i + 1]), [obb[i // 2]], [onb, ss8])
        k.actf(ss8, ss8.ap, ss8, ss8.ap, AF.Sqrt, bias=epsc.ap[:, 0:1], scale=1.0 / DV, extra_r=[epsc])
        k.op(DVE, lambda: nc.vector.reciprocal(out=rs8.ap, in_=ss8.ap), [ss8], [rs8])
        for i in range(8):
            k.op(ACT, lambda: nc.scalar.activation(out=onb.ap[:, i, :], in_=obap[:, i * 256:(i + 1) * 256], func=AF.Copy,
                                                   scale=rs8.ap[:, i:i + 1]), [obb[i // 2], rs8], [onb])
        yield
        for i in range(8):
            for j in range(2):
                c_ = i * 2 + j
                k.tr(psb_t[c_ // 8], psb_ap[:, c_ * 128:(c_ + 1) * 128], onb, onb.ap[:, i, j * 128:(j + 1) * 128],
                     identb, identb.ap, mark=(c_ % 8 == 7))
        yield
        p4 = psb_ap.rearrange("p (i j n) -> p i j n", j=2, n=128)
        for j in range(2):
            c = h * 2 + j
            k.stt(ogT, ogT.ap[:, c, :].rearrange("p (i n) -> p i n", n=128), psb_t[0], p4[:, :, j, :], vecs.ap[:, 32 + c:33 + c],
                  B["srT"], B["srT"].ap[:, j, :].rearrange("p (i n) -> p i n", n=128), ALU.mult, ALU.mult,
                  extra_r=[vecs, psb_t[1]])
        yield

    def p2_xown():
        gens = []
        for u in range(4):
            w_ = wload(w_in, w3(O_F + u * 256, 256), 16, 256)
            gens.append(proj_tm_g(hTo, range(8), w_[0], w_[1], 16, 256,
                                  lambda i, b, u=u: k.copy(ACT if i % 2 else DVE, X_own, X_own.ap[:, i, u * 256:(u + 1) * 256],
                                                           b, b.ap[:, 0:256])))
        return chain(*gens)

    run(p2_proj(0))
    for h in range(NH):
        filler = p2_proj(h + 1) if h + 1 < NH else None
        interleave(p2_gla(h), filler, 2)
    tap("ogT", ogT, ogT.ap, [128, 8, T_OWN], BF16)
    k.free(lrTo, qd, kd, Sp, smk, onb, mask8, *[t for d_ in hb for t in d_.values()], *gt_.values())
    for h in range(NH):
        for d in range(2):
            k.free(*Spp[h][d])
    if stop_after == "GLA":
        return finish()
    fT = k.alloc("fT", [128, 8, T_OWN], BF16)
    X_own = k.alloc("X_own", [128, 8, 1024], BF16)
    X_oth = k.alloc("X_oth2", [128, 8, 1024], BF16)
    run(p2_xown())
    k.dma(SP, X_oth, X_oth.ap.rearrange("p a n -> p (a n)"), X_oth_d, X_oth_d.ap)
    dftc = k.alloc("dftc", [128, 2, 2, 256], BF16)
    ctt = k.alloc("ctt", [128, 16, 256], BF16)
    stt_ = k.alloc("stt", [128, 16, 256], BF16)
    V1 = k.alloc("V1", [128, 8, 256], BF16)
    V2 = k.alloc("V2", [128, 8, 256], BF16)
    k.dma(SP, dftc, dftc.ap, c_dftc, c_dftc.ap)
    for tp in range(4):
        tps = slice(tp * 256, (tp + 1) * 256)
        k.dma(SP, ctt, ctt.ap, c_ct, c_ct.ap.rearrange("(tt p) n -> p tt n", p=128)[:, :, tps])
        k.dma(SP, stt_, stt_.ap, c_st, c_st.ap.rearrange("(tt p) n -> p tt n", p=128)[:, :, tps])
        for cc in range(8):
            for mt, Vt in ((ctt, V1), (stt_, V2)):
                b = psr.get()
                for tt_ in range(16):
                    Xs = X_own if tt_ < 8 else X_oth
                    k.mm(b, b.ap[:, 0:256], Xs, Xs.ap[:, tt_ % 8, cc * 128:(cc + 1) * 128], mt, mt.ap[:, tt_, :],
                         tt_ == 0, tt_ == 15)
                k.copy(ACT if Vt is V1 else DVE, Vt, Vt.ap[:, cc, :], b, b.ap[:, 0:256])
        for g in range(4):
            for j in range(2):
                b = psr.get()
                n = 0
                for s_, Vt in ((0, V1), (1, V2)):
                    for ci in range(2):
                        k.mm(b, b.ap[:, 0:256], dftc, dftc.ap[:, s_, ci, j * 128:(j + 1) * 128], Vt, Vt.ap[:, g * 2 + ci, :],
                             n == 0, n == 3)
                        n += 1
                k.copy(ACT if j else DVE, fT, fT.ap[:, g * 2 + j, tps], b, b.ap[:, 0:256])
    tap("fT", fT, fT.ap, [128, 8, T_OWN], BF16)
    k.free(X_own, dftc, ctt, stt_, V1, V2, X_oth)
    if stop_after == "FNET":
        return finish()
    yT = k.alloc("yT", [128, 16, T_OWN], BF16)
    sga = [k.alloc(f"sga{i}", [128, 512], F32) for i in range(2)]
    sgb = [k.alloc(f"sgb{i}", [128, 512], F32) for i in range(2)]
    it = 0
    for np_ in range(8):
        wa, wav = wload(w_in, w3(O_GA + np_ * 256, 256), 16, 256)
        wb_, wbv = wload(w_in, w3(O_GB + np_ * 256, 256), 16, 256)
        wo = wring.get()
        wov = wo.ap.rearrange("p (a kt n) -> p a kt n", a=2, n=256)
        k.dma(POOL, wo, wov[:, 0], w_gla_out, wsrc(w_gla_out, 8, np_ * 256, 256), join=True)
        k.dma(POOL, wo, wov[:, 1], w_fnet_out, wsrc(w_fnet_out, 8, np_ * 256, 256), join=True)
        for c in range(2):
            n = np_ * 2 + c
            cs_ = slice(c * 128, (c + 1) * 128)
            for t0 in (0, 512):
                ts_ = slice(t0, t0 + 512)
                bga, bgb, bya, byb = psr.get(), psr.get(), psr.get(), psr.get()
                for kt in range(16):
                    k.mm(bga, bga.ap, wa, wav[:, kt, cs_], hTo, hTo.ap[:, kt, ts_], kt == 0, kt == 15)
                for kt in range(16):
                    k.mm(bgb, bgb.ap, wb_, wbv[:, kt, cs_], hTo, hTo.ap[:, kt, ts_], kt == 0, kt == 15)
                for kt in range(8):
                    k.mm(bya, bya.ap, wo, wov[:, 0, kt, cs_], ogT, ogT.ap[:, kt, ts_], kt == 0, kt == 7)
                for kt in range(8):
                    k.mm(byb, byb.ap, wo, wov[:, 1, kt, cs_], fT, fT.ap[:, kt, ts_], kt == 0, kt == 7)
                a_, b_ = sga[it % 2], sgb[it % 2]
                it += 1
                k.actf(a_, a_.ap, bga, bga.ap, AF.Sigmoid)
                k.actf(b_, b_.ap, bgb, bgb.ap, AF.Sigmoid)
                k.tt(DVE, a_, a_.ap, a_, a_.ap, bya, bya.ap, ALU.mult)
                k.tt(DVE, b_, b_.ap, b_, b_.ap, byb, byb.ap, ALU.mult)
                k.tt(DVE, yT, yT.ap[:, n, ts_], a_, a_.ap, b_, b_.ap, ALU.add)
    tap("yT", yT, yT.ap, [128, 16, T_OWN], BF16)
    k.free(ogT, fT, *sga, *sgb)
    if stop_after == "MERGE":
        return finish()

    def resid_proj(actT, KT, w_t, gt_col0, src_t, dst_t):
        wbufs = None
        if KT * 256 > WB:
            wbufs = [k.alloc(f"wd{i}", [128, KT * 256], BF16) for i in range(2)]
        xg = [k.alloc(f"xg{i}", [128, 8, 256], F32) for i in range(2)]
        og = [k.alloc(f"og{i}", [128, 8, 256], F32) for i in range(2)]
        gtb = k.alloc("gtb", [128, D], F32)
        gbb = k.alloc("gbb", [128, D], F32)
        k.dma(SP, gtb, gtb.ap, mod_d, mod_d.ap[0:1, gt_col0:gt_col0 + D].partition_broadcast(128))
        k.dma(SP, gbb, gbb.ap, b_ada, b_ada.ap[0:1, gt_col0:gt_col0 + D].partition_broadcast(128))
        k.tt(DVE, gtb, gtb.ap, gtb, gtb.ap, gbb, gbb.ap, ALU.add)
        for dg in range(8):
            cs_ = slice(dg * 256, (dg + 1) * 256)
            if wbufs is None:
                wb, wv = wload(w_t, wsrc(w_t, KT, dg * 256, 256), KT, 256)
            else:
                wb = wbufs[dg % 2]
                wv = wb.ap.rearrange("p (kt n) -> p kt n", n=256)
                half = KT // 2
                k.dma(POOL, wb, wv[:, 0:half], w_t, wsrc(w_t, half, dg * 256, 256), join=True)
                k.dma(POOL, wb, wv[:, half:KT], w_t, wsrc(w_t, KT - half, dg * 256, 256, r0=half * 128), join=True)
            x_ = xg[dg % 2]
            o_ = og[dg % 2]
            k.dma(SP, x_, x_.ap, src_t, src_t.ap.rearrange("(i p) n -> p i n", p=128)[:, :, cs_])
            for i in range(8):
                b = psr.get()
                for kt in range(KT):
                    k.mm(b, b.ap[:, 0:256], actT, actT.ap[:, kt, i * 128:(i + 1) * 128], wb, wv[:, kt, :], kt == 0, kt == KT - 1)
                k.tt(DVE, o_, o_.ap[:, i, :], b, b.ap[:, 0:256], gtb, gtb.ap[:, cs_], ALU.mult)
                k.tt(DVE, o_, o_.ap[:, i, :], o_, o_.ap[:, i, :], x_, x_.ap[:, i, :], ALU.add)
            k.dma(SP, dst_t, dst_t.ap.rearrange("(i p) n -> p i n", p=128)[:, :, cs_], o_, o_.ap, join=True)
        k.free(gtb, gbb, *xg, *og)
        if wbufs is not None:
            k.free(*wbufs)

    resid_proj(yT, 16, w_out, 2 * D, xo, x1_d)
    k.free(yT)
    build_hT(x1_d, 8, hTo, lambda i: (4, 5))
    tap("h2T", hTo, hTo.ap, [128, 16, T_OWN], BF16)
    if stop_after == "WOUT":
        return finish()

    hid = k.alloc("hid", [128, 44, T_OWN], BF16)
    cvb = [k.alloc(f"cvb{i}", [128, T_OWN], F32) for i in range(2)]
    cgb = [k.alloc(f"cgb{i}", [128, T_OWN], F32) for i in range(2)]
    pairs = [(banks[0], banks[1], ps_h[:, 0:1024]), (banks[2], banks[3], ps_h[:, 1024:2048]),
             (banks[4], banks[5], ps_h[:, 2048:3072]), (banks[6], banks[7], ps_h[:, 3072:4096])]
    pi = 0
    for st in range(22):
        wvl, wvv = wload(w_up, wsrc(w_up, 16, st * 256, 256), 16, 256)
        wgt, wgv = wload(w_up, wsrc(w_up, 16, D_FF + st * 256, 256), 16, 256)
        for c in range(2):
            m = st * 2 + c
            res = []
            for (wt_, wv_, cb_, pcol) in ((wvl, wvv, cvb[m % 2], m), (wgt, wgv, cgb[m % 2], 44 + m)):
                b0, b1, pap = pairs[pi % 4]
                pi += 1
                for hf, bb in ((0, b0), (1, b1)):
                    for kt in range(16):
                        k.mm(bb, pap[:, hf * 512:(hf + 1) * 512], wt_, wv_[:, kt, c * 128:(c + 1) * 128],
                             hTo, hTo.ap[:, kt, hf * 512:(hf + 1) * 512], kt == 0, kt == 15)
                k.op(ACT, lambda: nc.scalar.activation(out=cb_.ap, in_=pap, func=AF.Identity,
                                                       bias=vecs.ap[:, 40 + pcol:41 + pcol],
                                                       scale=cw.ap[:, 1, pcol:pcol + 1]),
                     [b0, b1, vecs, cw], [cb_])
                p3 = pap.rearrange("p (r w) -> p r w", w=64)
                c3 = cb_.ap.rearrange("p (r w) -> p r w", w=64)
                k.op(DVE, lambda: nc.vector.scalar_tensor_tensor(out=c3[:, :, 1:64], in0=p3[:, :, 0:63],
                                                                scalar=cw.ap[:, 0, pcol:pcol + 1], in1=c3[:, :, 1:64],
                                                                op0=ALU.mult, op1=ALU.add),
                     [b0, b1, cw, cb_], [cb_])
                k.op(DVE, lambda: nc.vector.scalar_tensor_tensor(out=c3[:, :, 0:63], in0=p3[:, :, 1:64],
                                                                scalar=cw.ap[:, 2, pcol:pcol + 1], in1=c3[:, :, 0:63],
                                                                op0=ALU.mult, op1=ALU.add),
                     [b0, b1, cw, cb_], [cb_])
                res.append(cb_)
            cv_, cg_ = res
            k.actf(cg_, cg_.ap, cg_, cg_.ap, AF.Silu)
            k.tt(DVE, hid, hid.ap[:, m, :], cg_, cg_.ap, cv_, cv_.ap, ALU.mult)
    tap("hid", hid, hid.ap, [128, 44, T_OWN], BF16)
    k.free(*cvb, *cgb, hTo, *wbufs4)
    if stop_after == "UP":
        return finish()
    resid_proj(hid, 44, w_down, 5 * D, x1_d, x2_d)
    k.free(hid)
    gfb = k.alloc("gfb", [128, D], F32)
    xb = [k.alloc(f"fxb{i}", [128, D], F32) for i in range(3)]
    xq = [k.alloc(f"fxq{i}", [128, D], F32) for i in range(2)]
    k.dma(SP, gfb, gfb.ap, g_final, g_final.ap[0:1, :].partition_broadcast(128))
    for i in range(min(2, 8)):
        k.dma(SP, xb[i % 3], xb[i % 3].ap, x2_d, x2_d.ap[i * 128:(i + 1) * 128, :])
    for i in range(8):
        xt = xb[i % 3]
        xs = xq[i % 2]
        sm = sm_slots[i % 2]
        if i + 2 < 8:
            k.dma(SP, xb[(i + 2) % 3], xb[(i + 2) % 3].ap, x2_d, x2_d.ap[(i + 2) * 128:(i + 3) * 128, :])
        k.actf(xs, xs.ap, xt, xt.ap, AF.Square, accum=sm.ap[:, 0:1], extra_w=[sm])
        k.actf(sm, sm.ap[:, 1:2], sm, sm.ap[:, 0:1], AF.Sqrt, bias=epsc.ap[:, 0:1], scale=1.0 / D, extra_r=[epsc])
        k.op(DVE, lambda: nc.vector.reciprocal(out=sm.ap[:, 2:3], in_=sm.ap[:, 1:2]), [sm], [sm])
        k.actf(xs, xs.ap, xt, xt.ap, AF.Copy, scale=sm.ap[:, 2:3], extra_r=[sm])
        hD = D // 2
        k.tt(DVE, xs, xs.ap[:, 0:hD], xs, xs.ap[:, 0:hD], gfb, gfb.ap[:, 0:hD], ALU.mult)
        k.tt(POOL, xs, xs.ap[:, hD:D], xs, xs.ap[:, hD:D], gfb, gfb.ap[:, hD:D], ALU.mult)
        k.dma(SP, out_d, out_d.ap[i * 128:(i + 1) * 128, :], xs, xs.ap, join=True)
    return finish()


def _consts(half):
    c = {}
    c["c_identf"] = np.eye(128, dtype=np.float32)
    c["c_identb"] = np.eye(128).astype(ml_dtypes.bfloat16)
    t = np.arange(128)
    le = (t[:, None] <= t[None, :]).astype(np.float32)
    gt = (t[:, None] > t[None, :]).astype(np.float32)
    ge = (t[:, None] >= t[None, :]).astype(np.float32)
    lt = (t[:, None] < t[None, :]).astype(np.float32)
    c["c_tri"] = np.ascontiguousarray(np.stack([le, gt, ge, lt], axis=1) * np.float32(-1.0 / 16.0)).astype(np.float32)
    c["c_mask"] = np.ascontiguousarray(np.stack([le, ge], axis=1)).astype(np.float32)
    c["c_mask8"] = np.ascontiguousarray(np.tile(np.concatenate([le, ge], axis=1), (1, 8))).astype(ml_dtypes.bfloat16)
    cc = np.arange(256)
    ang = 2 * np.pi * (np.outer(cc, cc) % 256) / 256
    Cc, Sc = np.cos(ang), np.sin(ang)
    dc = np.stack([Cc.reshape(2, 128, 256), (-Sc).reshape(2, 128, 256)], axis=0)
    c["c_dftc"] = np.ascontiguousarray(dc.transpose(2, 0, 1, 3)).astype(ml_dtypes.bfloat16)
    own = np.arange(1024) + half * 1024
    oth = np.arange(1024) + (1 - half) * 1024
    tt = np.concatenate([own, oth])
    ang = 2 * np.pi * (np.outer(tt, own) % 2048) / 2048
    sc = 1.0 / np.sqrt(2048.0 * 256.0)
    c["c_ct"] = (np.cos(ang) * sc).astype(ml_dtypes.bfloat16)
    c["c_st"] = (np.sin(ang) * sc).astype(ml_dtypes.bfloat16)
    fl = np.zeros((128, 2), np.float32)
    fl[:, 0] = 1.0 if half == 1 else 0.0
    fl[:, 1] = 1.0 if half == 0 else 0.0
    c["c_flags"] = fl
    return c


def make_in_maps(x, c, ctx, c_ctx, w_ada, b_ada, g_norm1, w_in, w_gate_f, b_gate_f, w_gate_b, b_gate_b, g_gla,
                 w_gla_out, w_fnet_out, w_out, g_norm2, w_up, conv_w, conv_b, w_down, g_final):
    f = lambda a: np.ascontiguousarray(np.asarray(a, dtype=np.float32))
    x, c, ctx, c_ctx = f(x), f(c), f(ctx), f(c_ctx)
    shared = {
        "w_ada": f(w_ada[0]), "b_ada": f(b_ada[0]).reshape(1, -1), "w_in": f(w_in[0]),
        "wgate": np.ascontiguousarray(np.stack([f(w_gate_f[0]), f(w_gate_b[0])], axis=0)),
        "bgate": np.ascontiguousarray(np.stack([f(b_gate_f[0]), f(b_gate_b[0])], axis=0)),
        "vtab": np.ascontiguousarray(np.concatenate([f(g_norm1[0]).reshape(16, 128), f(g_norm2[0]).reshape(16, 128),
                                                     f(g_gla[0]).reshape(8, 128), f(conv_b[0]).reshape(88, 128)], axis=0)),
        "cwtab": np.ascontiguousarray(f(conv_w[0]).reshape(3, 88, 128)),
        "g_final": f(g_final).reshape(1, -1),
        "w_gla_out": f(w_gla_out[0]), "w_fnet_out": f(w_fnet_out[0]), "w_out": f(w_out[0]),
        "w_up": f(w_up[0]), "w_down": f(w_down[0]),
    }
    consts = [_consts(0), _consts(1)]
    maps = []
    for core in range(8):
        b, half = core // 2, core % 2
        m = dict(shared)
        m.update(consts[half])
        m["xo"] = np.ascontiguousarray(x[b, half * 1024:(half + 1) * 1024])
        m["xr"] = np.ascontiguousarray(np.concatenate([x[b, (1 - half) * 1024:(2 - half) * 1024], ctx[b]], axis=0))
        m["cvec"] = np.ascontiguousarray(np.stack([c[b], c_ctx], axis=0))
        maps.append(m)
    return maps


def kernel(**inputs):
    maps = make_in_maps(**inputs)
    nc = build()
    res = run_bass_kernel_spmd(nc, maps, core_ids=list(range(8)))
    out = np.zeros((4, 2048, 2048), np.float32)
    for core in range(8):
        b, half = core // 2, core % 2
        out[b, half * 1024:(half + 1) * 1024] = res.results[core]["out"]
    return out
```
